# Optimizing a Trainium2 kernel written in Bass

```python
import math
import jax, jax.numpy as jnp
from jax import lax
import numpy as np

D_MODEL = 1024
BATCH = 4
SEQ = 4096
DEPTH = 2

N_MOD = 6
EPS = 1e-6
SSM_EXPAND = 2
SSM_D_INNER = SSM_EXPAND * D_MODEL
SSM_HEAD_DIM = 64
SSM_HEADS = SSM_D_INNER // SSM_HEAD_DIM
SSM_GROUPS = 4
SSM_STATE = 128
SSM_CONV = 4
SSM_CHUNK = 128
SSM_CONV_DIM = SSM_D_INNER + 2 * SSM_GROUPS * SSM_STATE
ATT_HEAD_DIM = 64
ATT_KV_HEADS = D_MODEL // ATT_HEAD_DIM
ATT_PATTERNS = ((128, 1), (512, 4), (2048, 16))
ATT_GROUPS = len(ATT_PATTERNS)
ATT_Q_HEADS = ATT_GROUPS * ATT_KV_HEADS
ATT_WIDTH = ATT_KV_HEADS * ATT_HEAD_DIM
REL_BUCKETS = 32
REL_MAX_DIST = 2048
HY_SIZES = (SSM_D_INNER, SSM_CONV_DIM, SSM_HEADS, ATT_Q_HEADS * ATT_HEAD_DIM, ATT_WIDTH, ATT_WIDTH)
HY_SPLITS = tuple(int(v) for v in np.cumsum(HY_SIZES)[:-1])
HY_IN_DIM = sum(HY_SIZES)
HY_OUT_DIM = SSM_D_INNER + ATT_WIDTH
CONV_WIDTH = 31
FFN_HIDDEN = -(-8 * D_MODEL // (3 * 256)) * 256
N_EVEN = (DEPTH + 1) // 2
N_ODD = DEPTH // 2

kernel_name = "hybrid_ssd_dilated_conformer_block"


def rms_norm(x, g):
    xf = x.astype(jnp.float32)
    y = xf * lax.rsqrt(jnp.mean(xf * xf, -1, keepdims=True) + EPS)
    return (y * g.astype(jnp.float32)).astype(x.dtype)


def layer_norm(x, g, b):
    xf = x.astype(jnp.float32)
    mu = jnp.mean(xf, -1, keepdims=True)
    var = jnp.mean(jnp.square(xf - mu), -1, keepdims=True)
    y = (xf - mu) * lax.rsqrt(var + EPS)
    return (y * g.astype(jnp.float32) + b.astype(jnp.float32)).astype(x.dtype)


def causal_depthwise_conv(x, w, b):
    k = w.shape[0]
    out = lax.conv_general_dilated(
        x, w[:, None, :].astype(x.dtype), window_strides=(1,), padding=[(k - 1, 0)],
        dimension_numbers=('NWC', 'WIO', 'NWC'), feature_group_count=x.shape[-1])
    return out + b


def t5_bucket(dist):
    max_exact = REL_BUCKETS // 2
    n = jnp.maximum(dist, 1).astype(jnp.float32)
    large = max_exact + jnp.log(n / max_exact) / math.log(REL_MAX_DIST / max_exact) * (REL_BUCKETS - max_exact)
    large = jnp.minimum(large.astype(jnp.int32), REL_BUCKETS - 1)
    return jnp.where(dist < max_exact, dist, large)


def ssd_chunked(x, dt, A, Bm, Cm):
    f32 = jnp.float32
    b_, s, h, p = x.shape
    g, n = Bm.shape[2], Bm.shape[3]
    r, q = h // g, SSM_CHUNK
    nc = s // q
    x = x.astype(f32).reshape(b_, nc, q, g, r, p)
    dt = dt.astype(f32).reshape(b_, nc, q, g, r)
    Bm = Bm.astype(f32).reshape(b_, nc, q, g, n)
    Cm = Cm.astype(f32).reshape(b_, nc, q, g, n)
    a_cs = jnp.cumsum(dt * A.astype(f32).reshape(g, r), axis=2)
    xdt = x * dt[..., None]
    seg = a_cs[:, :, :, None] - a_cs[:, :, None, :]
    causal = jnp.tril(jnp.ones((q, q), bool))[:, :, None, None]
    decay = jnp.exp(jnp.where(causal, seg, -jnp.inf))
    cb = jnp.einsum('bclgn,bcsgn->bclsg', Cm, Bm)
    y_diag = jnp.einsum('bclsgr,bcsgrp->bclgrp', cb[..., None] * decay, xdt)
    decay_end = jnp.exp(a_cs[:, :, -1:] - a_cs)
    states = jnp.einsum('bcsgn,bcsgrp->bcgrpn', Bm, xdt * decay_end[..., None])
    chunk_decay = jnp.exp(a_cs[:, :, -1])

    def step(hs, inp):
        st, dec = inp
        return dec[..., None, None] * hs + st, hs

    h0 = jnp.zeros((b_, g, r, p, n), f32)
    _, prev = lax.scan(step, h0, (jnp.moveaxis(states, 1, 0), jnp.moveaxis(chunk_decay, 1, 0)))
    prev = jnp.moveaxis(prev, 0, 1)
    y_off = jnp.einsum('bclgn,bcgrpn->bclgrp', Cm, prev) * jnp.exp(a_cs)[..., None]
    return (y_diag + y_off).reshape(b_, s, h, p)


def dilated_branch(q, k, v, bias_tab, window, dil):
    b_, s, h, dh = q.shape
    blk = window // dil
    L = s // dil
    nb = -(-L // blk)
    lp = nb * blk

    def blocks(t):
        t = t.reshape(b_, L, dil, h, dh)
        t = jnp.pad(t, ((0, 0), (0, lp - L), (0, 0), (0, 0), (0, 0)))
        return t.reshape(b_, nb, blk, dil, h, dh)

    def band_keys(t):
        prev = jnp.pad(t, ((0, 0), (1, 0), (0, 0), (0, 0), (0, 0), (0, 0)))[:, :-1]
        return jnp.concatenate([prev, t], axis=2)

    qb = blocks(q)
    kw, vw = band_keys(blocks(k)), band_keys(blocks(v))
    i = jnp.arange(blk)[:, None]
    j = jnp.arange(2 * blk)[None, :]
    delta = blk + i - j
    band = (delta >= 0) & (delta <= blk)
    kpos = jnp.arange(nb)[:, None] * blk + jnp.arange(2 * blk)[None, :] - blk
    mask = band[None] & (kpos >= 0)[:, None, :]
    bias = jnp.transpose(bias_tab[t5_bucket(jnp.maximum(delta, 0) * dil)], (2, 0, 1)).astype(jnp.float32)
    sc = jnp.einsum('bnirhd,bnjrhd->bnrhij', qb, kw).astype(jnp.float32) * (dh ** -0.5) + bias
    sc = jnp.where(mask[None, :, None, None], sc, -jnp.inf)
    m = jnp.max(sc, -1, keepdims=True)
    pr = jnp.exp(sc - m)
    l = jnp.sum(pr, -1, keepdims=True)
    o = jnp.einsum('bnrhij,bnjrhd->bnirhd', (pr / l).astype(v.dtype), vw)
    lse = jnp.transpose((m + jnp.log(l))[..., 0], (0, 1, 4, 2, 3))
    lse = lse.reshape(b_, lp, dil, h)[:, :L].reshape(b_, s, h)
    o = o.reshape(b_, lp, dil, h, dh)[:, :L].reshape(b_, s, h, dh)
    return o, lse


def dilated_attention(q, k, v, rel_table):
    b_, s, _, h, dh = q.shape
    outs, lses = [], []
    for gi, (w, d) in enumerate(ATT_PATTERNS):
        o, lse = dilated_branch(q[:, :, gi], k, v, rel_table[:, gi * h:(gi + 1) * h], w, d)
        outs.append(o)
        lses.append(lse)
    wgt = jax.nn.softmax(jnp.stack(lses, 0), axis=0)
    o = jnp.einsum('gbsh,gbshd->bshd', wgt.astype(outs[0].dtype), jnp.stack(outs, 0))
    return o.reshape(b_, s, h * dh)


def hybrid_mixer(h, w_in, conv_w, conv_b, dt_bias, a_log, d_skip, ssm_norm_g, w_out, rel_table):
    b_, s, _ = h.shape
    z, xbc, dt_raw, q, k, v = jnp.split(h @ w_in, HY_SPLITS, axis=-1)
    xbc = jax.nn.silu(causal_depthwise_conv(xbc, conv_w, conv_b))
    xs, bm, cm = jnp.split(xbc, (SSM_D_INNER, SSM_D_INNER + SSM_GROUPS * SSM_STATE), axis=-1)
    xs = xs.reshape(b_, s, SSM_HEADS, SSM_HEAD_DIM)
    dt = jax.nn.softplus((dt_raw + dt_bias).astype(jnp.float32))
    A = -jnp.exp(a_log.astype(jnp.float32))
    y = ssd_chunked(xs, dt, A, bm.reshape(b_, s, SSM_GROUPS, SSM_STATE), cm.reshape(b_, s, SSM_GROUPS, SSM_STATE))
    y = y + d_skip.astype(jnp.float32)[:, None] * xs.astype(jnp.float32)
    y = y.reshape(b_, s, SSM_D_INNER).astype(h.dtype)
    y = rms_norm(y * jax.nn.silu(z), ssm_norm_g)
    att = dilated_attention(q.reshape(b_, s, ATT_GROUPS, ATT_KV_HEADS, ATT_HEAD_DIM),
                            k.reshape(b_, s, ATT_KV_HEADS, ATT_HEAD_DIM),
                            v.reshape(b_, s, ATT_KV_HEADS, ATT_HEAD_DIM), rel_table)
    return jnp.concatenate([y, att.astype(y.dtype)], axis=-1) @ w_out


def conformer_conv(h, w1, b1, w_dw, b_dw, ln_g, ln_b, w2, b2):
    a, gt = jnp.split(h @ w1 + b1, 2, axis=-1)
    u = a * jax.nn.sigmoid(gt)
    u = causal_depthwise_conv(u, w_dw, b_dw)
    u = jax.nn.silu(layer_norm(u, ln_g, ln_b))
    return u @ w2 + b2


def swiglu(h, wg, wu, wd):
    return (jax.nn.silu(h @ wg) * (h @ wu)) @ wd


def setup_inputs(seed: int = 0) -> dict:
    key = jax.random.key(seed)
    ks = iter(jax.random.split(key, 40))
    f32 = jnp.float32

    def nrm(shape, scale=1.0):
        return jax.random.normal(next(ks), shape, f32) * scale

    D = D_MODEL
    dt0 = jnp.exp(jax.random.uniform(next(ks), (N_EVEN, SSM_HEADS), f32, math.log(1e-3), math.log(1e-1)))
    return {
        "x": nrm((BATCH, SEQ, D)),
        "c": nrm((BATCH, D)),
        "ada_w": nrm((DEPTH, D, N_MOD * D), 0.5 * D ** -0.5),
        "ada_b": nrm((DEPTH, N_MOD * D), 0.02),
        "norm_mix_g": 1.0 + nrm((DEPTH, D), 0.05),
        "norm_ffn_g": 1.0 + nrm((DEPTH, D), 0.05),
        "hy_w_in": nrm((N_EVEN, D, HY_IN_DIM), D ** -0.5),
        "hy_conv_w": nrm((N_EVEN, SSM_CONV, SSM_CONV_DIM), SSM_CONV ** -0.5),
        "hy_conv_b": nrm((N_EVEN, SSM_CONV_DIM), 0.02),
        "hy_dt_bias": dt0 + jnp.log(-jnp.expm1(-dt0)),
        "hy_a_log": jnp.log(jax.random.uniform(next(ks), (N_EVEN, SSM_HEADS), f32, 1.0, 16.0)),
        "hy_d_skip": 1.0 + nrm((N_EVEN, SSM_HEADS), 0.1),
        "hy_ssm_norm_g": 1.0 + nrm((N_EVEN, SSM_D_INNER), 0.05),
        "hy_w_out": nrm((N_EVEN, HY_OUT_DIM, D), HY_OUT_DIM ** -0.5),
        "rel_table": nrm((REL_BUCKETS, ATT_Q_HEADS), 0.2),
        "cv_w_pw1": nrm((N_ODD, D, 2 * D), D ** -0.5),
        "cv_b_pw1": nrm((N_ODD, 2 * D), 0.02),
        "cv_w_dw": nrm((N_ODD, CONV_WIDTH, D), CONV_WIDTH ** -0.5),
        "cv_b_dw": nrm((N_ODD, D), 0.02),
        "cv_ln_g": 1.0 + nrm((N_ODD, D), 0.05),
        "cv_ln_b": nrm((N_ODD, D), 0.02),
        "cv_w_pw2": nrm((N_ODD, D, D), D ** -0.5),
        "cv_b_pw2": nrm((N_ODD, D), 0.02),
        "ffn_w_gate": nrm((DEPTH, D, FFN_HIDDEN), D ** -0.5),
        "ffn_w_up": nrm((DEPTH, D, FFN_HIDDEN), D ** -0.5),
        "ffn_w_down": nrm((DEPTH, FFN_HIDDEN, D), FFN_HIDDEN ** -0.5),
        "final_norm_g": 1.0 + nrm((D,), 0.05),
    }


def reference(x, c, ada_w, ada_b, norm_mix_g, norm_ffn_g, hy_w_in, hy_conv_w, hy_conv_b, hy_dt_bias,
              hy_a_log, hy_d_skip, hy_ssm_norm_g, hy_w_out, rel_table, cv_w_pw1, cv_b_pw1, cv_w_dw,
              cv_b_dw, cv_ln_g, cv_ln_b, cv_w_pw2, cv_b_pw2, ffn_w_gate, ffn_w_up, ffn_w_down,
              final_norm_g):
    cs = jax.nn.silu(c)
    for i in range(DEPTH):
        mod = cs @ ada_w[i] + ada_b[i]
        sh1, sc1, g1, sh2, sc2, g2 = [m[:, None, :] for m in jnp.split(mod, N_MOD, axis=-1)]
        h = rms_norm(x, norm_mix_g[i]) * (1 + sc1) + sh1
        j = i // 2
        if i % 2 == 0:
            mix = hybrid_mixer(h, hy_w_in[j], hy_conv_w[j], hy_conv_b[j], hy_dt_bias[j], hy_a_log[j],
                               hy_d_skip[j], hy_ssm_norm_g[j], hy_w_out[j], rel_table)
        else:
            mix = conformer_conv(h, cv_w_pw1[j], cv_b_pw1[j], cv_w_dw[j], cv_b_dw[j], cv_ln_g[j],
                                 cv_ln_b[j], cv_w_pw2[j], cv_b_pw2[j])
        x = x + g1 * mix
        h = rms_norm(x, norm_ffn_g[i]) * (1 + sc2) + sh2
        x = x + g2 * swiglu(h, ffn_w_gate[i], ffn_w_up[i], ffn_w_down[i])
    return rms_norm(x, final_norm_g)
```

```python
import math
import numpy as np
import concourse.bass as bass
import concourse.mybir as mybir
from concourse.bass_utils import run_bass_kernel_spmd
from contextlib import ExitStack

F32 = mybir.dt.float32
BF16 = mybir.dt.bfloat16
AF = mybir.ActivationFunctionType
ALU = mybir.AluOpType
AX = mybir.AxisListType

D = 1024
EPS = 1e-6
TE = 4096
OT0 = 15
NOT = 17
TO = NOT * 128
U0 = OT0 * 128
FFH = 2816
DILS = (1, 4, 16)
HY_SPL = (0, 2048, 5120, 5152, 8224, 9248, 10272)


class Buf:
    __slots__ = ("w", "r", "name", "wl")

    def __init__(self, name=""):
        self.w = None
        self.r = []
        self.wl = []
        self.name = name


class T:
    __slots__ = ("ap", "buf")

    def __init__(self, ap, buf):
        self.ap = ap
        self.buf = buf

    def __getitem__(self, key):
        return T(self.ap[key], self.buf)

    def re(self, s, **kw):
        return T(self.ap.rearrange(s, **kw), self.buf)

    def sub(self, ap_or_key, name=""):
        return T(self.ap[ap_or_key], Buf(name))


NSLOT = 8


class MK:
    def __init__(self, nc, es):
        self.nc = nc
        self.es = es
        self.eng = {"pe": nc.tensor, "act": nc.scalar, "dve": nc.vector, "pool": nc.gpsimd, "sp": nc.sync}
        self.prog = {k: [] for k in self.eng}
        self.sem = {}
        self.cnt = {}
        for k in self.eng:
            self.sem[k] = es.enter_context(nc.semaphore("s_" + k))
            self.cnt[k] = 0
        self.dsem = {}
        self.dcnt = {}
        self.dnext = {}
        for q in ("sp", "act", "pool"):
            self.dsem[q] = [es.enter_context(nc.semaphore("d_%s%d" % (q, i))) for i in range(NSLOT)]
            self.dcnt[q] = [0] * NSLOT
            self.dnext[q] = 0
        self.waited = {k: {} for k in self.eng}
        self.ninst = 0

    def _nm(self, name):
        self.uid = getattr(self, "uid", 0) + 1
        return "%s_u%d" % (name, self.uid)

    def sb(self, es, name, shape, dt):
        name = self._nm(name)
        t = es.enter_context(self.nc.sbuf_tensor(name, list(shape), dt))
        return T(t[:], Buf(name))

    def ps(self, es, name, shape, dt):
        name = self._nm(name)
        t = es.enter_context(self.nc.psum_tensor(name, list(shape), dt))
        return T(t[:], Buf(name))

    def dram(self, name, shape, dt, kind="Internal"):
        t = self.nc.dram_tensor(name, list(shape), dt, kind=kind)
        return T(t.ap(), Buf(name))

    def _semh(self, s):
        return self.sem[s[1]] if s[0] == "e" else self.dsem[s[1]][s[2]]

    def _emit(self, e, reads, writes, fn, inc=True, dma_q=None, par=False):
        deps = {}

        def add(d):
            if d is None:
                return
            s, v = d
            if deps.get(s, 0) < v:
                deps[s] = v

        mykey = ("e", e)
        for t in reads:
            add(t.buf.w)
            for w in t.buf.wl:
                add(w)
        for t in writes:
            w = t.buf.w
            if not par:
                if w is not None and not (w[0] == mykey and dma_q is None):
                    add(w)
                for w2 in t.buf.wl:
                    add(w2)
            elif w is not None and w[0][0] != "d":
                add(w)
            for r in t.buf.r:
                if not (r[0] == mykey and dma_q is None):
                    add(r)
        if dma_q is not None:
            q = dma_q
            slot = self.dnext[q]
            self.dnext[q] = (slot + 1) % NSLOT
            prev = self.dcnt[q][slot]
            if prev > 0:
                add((("d", q, slot), prev))
            self.dcnt[q][slot] = prev + 16
            myval = (("d", q, slot), prev + 16)
            semh = self.dsem[q][slot]
            incv = 16
            doinc = True
        else:
            if inc:
                self.cnt[e] += 1
                myval = (mykey, self.cnt[e])
            else:
                myval = (mykey, self.cnt[e] + 1)
            semh = self.sem[e]
            incv = 1
            doinc = inc
        waits = []
        wd = self.waited[e]
        for s, v in deps.items():
            if wd.get(s, 0) >= v:
                continue
            wd[s] = v
            waits.append((self._semh(s), v))

        def run(eh, waits=waits, fn=fn, semh=semh, incv=incv, doinc=doinc):
            for sh, v in waits:
                eh.wait_ge(sh, v)
            ins = fn(eh)
            if doinc:
                ins.then_inc(semh, incv)

        self.prog[e].append(run)
        self.ninst += 1
        wbufs = set(id(t.buf) for t in writes)
        for t in writes:
            if par:
                t.buf.wl.append(myval)
            else:
                t.buf.w = myval
                t.buf.wl = []
                t.buf.r = []
        for t in reads:
            if id(t.buf) not in wbufs:
                t.buf.r.append(myval)
                if len(t.buf.r) > 48:
                    m = {}
                    for s, v in t.buf.r:
                        if m.get(s, 0) < v:
                            m[s] = v
                    t.buf.r = list(m.items())
        return myval

    @staticmethod
    def _a(x):
        return x.ap if isinstance(x, T) else x

    def mm(self, out, lhsT, rhs, start=True, stop=True):
        self._emit("pe", [lhsT, rhs], [out],
                   lambda eh: eh.matmul(out.ap, lhsT.ap, rhs.ap, start=start, stop=stop), inc=stop)

    def transpose(self, out, in_, ident):
        self._emit("pe", [in_, ident], [out], lambda eh: eh.transpose(out.ap, in_.ap, ident.ap))

    def act(self, out, in_, func, bias=None, scale=1.0, accum=None):
        reads = [in_] + [x for x in (bias, scale) if isinstance(x, T)]
        writes = [out] + ([accum] if accum is not None else [])
        kw = {}
        if bias is not None:
            kw["bias"] = self._a(bias)
        if accum is not None:
            kw["accum_out"] = accum.ap
        sc = self._a(scale)
        self._emit("act", reads, writes, lambda eh: eh.activation(out.ap, in_.ap, func, scale=sc, **kw))

    def tt(self, e, out, in0, in1, op):
        self._emit(e, [in0, in1], [out], lambda eh: eh.tensor_tensor(out.ap, in0.ap, in1.ap, op))

    def ts(self, e, out, in0, s1, s2, op0, op1=None):
        reads = [in0] + [x for x in (s1, s2) if isinstance(x, T)]
        a1, a2 = self._a(s1), self._a(s2)
        if op1 is None:
            self._emit(e, reads, [out], lambda eh: eh.tensor_scalar(out.ap, in0.ap, a1, None, op0))
        else:
            self._emit(e, reads, [out], lambda eh: eh.tensor_scalar(out.ap, in0.ap, a1, a2, op0, op1))

    def stt(self, e, out, in0, scalar, in1, op0, op1):
        reads = [in0, in1] + ([scalar] if isinstance(scalar, T) else [])
        sc = self._a(scalar)
        self._emit(e, reads, [out], lambda eh: eh.scalar_tensor_tensor(out.ap, in0.ap, sc, in1.ap, op0, op1))

    def copy(self, e, out, in_):
        if e == "act":
            self._emit(e, [in_], [out], lambda eh: eh.copy(out.ap, in_.ap))
        else:
            self._emit(e, [in_], [out], lambda eh: eh.tensor_copy(out.ap, in_.ap))

    def memset(self, e, out, val):
        self._emit(e, [], [out], lambda eh: eh.memset(out.ap, val))

    def recip(self, out, in_):
        self._emit("dve", [in_], [out], lambda eh: eh.reciprocal(out.ap, in_.ap))

    def dma(self, out, in_, q="sp", par=False):
        return self._emit(q, [in_], [out], lambda eh: eh.dma_start(out=out.ap, in_=in_.ap), dma_q=q, par=par)

    def barrier(self):
        targets = [(("e", k), self.cnt[k]) for k in self.eng if self.cnt[k] > 0]
        for q in self.dsem:
            for s in range(NSLOT):
                if self.dcnt[q][s] > 0:
                    targets.append((("d", q, s), self.dcnt[q][s]))
        for e in self.eng:
            waits = []
            wd = self.waited[e]
            for s, v in targets:
                if s == ("e", e) or wd.get(s, 0) >= v:
                    continue
                wd[s] = v
                waits.append((self._semh(s), v))

            def run(eh, waits=waits):
                for sh, v in waits:
                    eh.wait_ge(sh, v)

            self.prog[e].append(run)

    def finish(self):
        self.barrier()
        prog = self.prog
        with self.nc.Block() as block:
            @block.sync
            def _(eh):
                for f in prog["sp"]:
                    f(eh)

            @block.tensor
            def _(eh):
                for f in prog["pe"]:
                    f(eh)

            @block.scalar
            def _(eh):
                for f in prog["act"]:
                    f(eh)

            @block.vector
            def _(eh):
                for f in prog["dve"]:
                    f(eh)

            @block.gpsimd
            def _(eh):
                for f in prog["pool"]:
                    f(eh)


def bc(t, shape_steps, off=0):
    return T(bass.AP(t.ap.tensor, off, [list(x) for x in shape_steps]), t.buf)


def _t5_bucket_np(dist):
    dist = np.asarray(dist, np.int64)
    n = np.maximum(dist, 1).astype(np.float64)
    large = 16 + np.log(n / 16.0) / math.log(2048 / 16) * 16
    large = np.minimum(np.floor(large + 1e-9).astype(np.int64), 31)
    return np.where(dist < 16, dist, large)


def host_consts():
    c = {}
    c["ident"] = np.eye(128, dtype=np.float32)
    c["jrev"] = np.ascontiguousarray(np.eye(128, dtype=np.float32)[::-1])
    k = np.arange(128)
    c["tri"] = (k[:, None] <= k[None, :]).astype(np.float32)
    c["ustr"] = (k[:, None] > k[None, :]).astype(np.float32)
    c["ones"] = np.ones((128, 128), np.float32)
    oh = np.zeros((3, 32, 384), np.float32)
    pm = np.zeros((3, 16, 384), np.float32)
    for g, d in enumerate(DILS):
        for u in range(129):
            b = int(_t5_bucket_np(u * d))
            oh[g, b, 128 + u] = 1.0
            pm[g, :, 128 + u] = 1.0
    c["onehot"] = oh
    c["pmask"] = pm
    return c


def build(upto=99, taps=()):
    nc = bass.Bass("TRN2", target_bir_lowering=False)
    es0 = ExitStack()
    with es0:
        mk = MK(nc, es0)
        taps = set(taps)

        def din(name, shape, dt=F32):
            return mk.dram(name, shape, dt, kind="ExternalInput")

        def dscr(name, shape, dt):
            return mk.dram(name, shape, dt, kind=("ExternalOutput" if name in taps else "Internal"))

        xe = din("xe", [TE, D])
        cb = din("cb", [128, 8])
        flag_d = din("flag", [128, 1])
        ada_w = din("ada_w", [2, D, 6 * D])
        ada_b = din("ada_b", [2, 6 * D])
        nmg = din("norm_mix_g", [2, D])
        nfg = din("norm_ffn_g", [2, D])
        w_in = din("hy_w_in", [D, 10272])
        convw = din("convw", [128, 24 * 4])
        convb = din("convb", [128, 24])
        dtb = din("hy_dt_bias", [1, 32])
        alog = din("hy_a_log", [1, 32])
        dskip = din("hy_d_skip", [1, 32])
        ssmg = din("hy_ssm_norm_g", [1, 2048])
        w_out = din("hy_w_out", [3072, D])
        relt = din("rel_table", [32, 48])
        w_pw1 = din("cv_w_pw1", [D, 2 * D])
        b_pw1 = din("b_pw1", [128, 16])
        w_dw = din("w_dw", [128, 8 * 31])
        b_dw = din("b_dw", [128, 8])
        ln_g = din("ln_g", [128, 8])
        ln_b = din("ln_b", [128, 8])
        w_pw2 = din("cv_w_pw2", [D, D])
        b_pw2 = din("cv_b_pw2", [1, D])
        w_gate = din("ffn_w_gate", [2, D, FFH])
        w_up = din("ffn_w_up", [2, D, FFH])
        w_down = din("ffn_w_down", [2, FFH, D])
        fng = din("final_norm_g", [1, D])
        ident_d = din("ident", [128, 128])
        jrev_d = din("jrev", [128, 128])
        tri_d = din("tri", [128, 128])
        ustr_d = din("ustr", [128, 128])
        ones_d = din("ones", [128, 128])
        onehot_d = din("onehot", [3, 32, 384])
        pmask_d = din("pmask", [3, 16, 384])
        yout = mk.dram("yout", [2048, D], F32, kind="ExternalOutput")

        XSB = dscr("XSB", [TE, 2560], BF16)
        BCF = dscr("BCF", [1024, TE], BF16)
        ZS = dscr("ZS", [TO, 2048], F32)
        Qd = dscr("Qd", [3, 1024, TO], BF16)
        Kd = dscr("Kd", [3, 1024, TE], BF16)
        VA = dscr("VA", [TE, 1056], BF16)
        YNT = dscr("YNT", [2048, TO], BF16)
        OA = dscr("OA", [3, TO, 1040], F32)
        ED = dscr("ED", [3, 128, 16 * 256], F32)
        PM = dscr("PM", [48, 384], F32)
        X1 = dscr("X1", [TO, D], F32)
        X2 = dscr("X2", [TO, D], F32)
        X3 = dscr("X3", [2048, D], F32)
        DBG = dscr("DBG", [128, 4096], F32)

        identf = mk.sb(es0, "identf", [128, 128], F32)
        identb = mk.sb(es0, "identb", [128, 128], BF16)
        jrev = mk.sb(es0, "jrev_s", [128, 128], F32)
        tri = mk.sb(es0, "tri_s", [128, 128], F32)
        ustr = mk.sb(es0, "ustr_s", [128, 128], F32)
        ones = mk.sb(es0, "ones_s", [128, 128], F32)
        flag = mk.sb(es0, "flag_s", [128, 1], F32)
        mk.dma(identf, ident_d)
        mk.dma(identb, ident_d, q="pool")
        mk.dma(jrev, jrev_d)
        mk.dma(tri, tri_d)
        mk.dma(ustr, ustr_d)
        mk.dma(ones, ones_d)
        mk.dma(flag, flag_d)
        cs = mk.sb(es0, "cs", [128, 8], F32)
        csb = mk.sb(es0, "csb", [128, 8, 128], F32)
        mk.dma(cs, cb)
        mk.act(cs, cs, AF.Silu)
        mk.copy("dve", csb, bc(cs, [[cs.ap.ap[0][0], 128], [1, 8], [0, 128]], cs.ap.offset))

        def wview(Wd, kc):
            return Wd.re("(k p) n -> p k n", p=128)

        def wload(dst, Wd, c0, n):
            mk.dma(dst, wview(Wd, 0)[:, :, c0:c0 + n], q="pool")

        def compute_mod(es, l):
            modt = mk.sb(es, "mod%d" % l, [128, 6, D], F32)
            with ExitStack() as esl:
                wa = [mk.sb(esl, "adaw%d" % i, [128, 8, 512], F32) for i in range(4)]
                pp = [mk.ps(esl, "modps%d" % i, [128, 512], F32) for i in range(4)]
                bt = mk.sb(esl, "adab", [128, 6 * D], F32)
                gt = mk.sb(esl, "ngt", [128, 2, D], F32)
                mk.dma(bt, bc(ada_b, [[0, 128], [1, 6 * D]], l * 6 * D))
                mk.dma(gt[:, 0, :], bc(nmg, [[0, 128], [1, D]], l * D))
                mk.dma(gt[:, 1, :], bc(nfg, [[0, 128], [1, D]], l * D))
                awl = T(ada_w.ap[l], ada_w.buf).re("(k p) n -> p k n", p=128)
                for nb in range(12):
                    w = wa[nb % 4]
                    mk.dma(w, awl[:, :, nb * 512:(nb + 1) * 512], q=("sp" if nb % 2 == 0 else "act"))
                    p = pp[nb % 4]
                    for k in range(8):
                        mk.mm(p, csb[:, k, :], w[:, k, :], start=(k == 0), stop=(k == 7))
                    mk.tt("dve", modt[:, nb // 2, (nb % 2) * 512:(nb % 2 + 1) * 512], p,
                          bt[:, nb * 512:(nb + 1) * 512], ALU.add)
                for (j, gi) in ((1, 0), (4, 1)):
                    mk.stt("dve", modt[:, j, :], modt[:, j, :], 1.0, gt[:, gi, :], ALU.add, ALU.mult)
                mk.barrier()
            return {"S1": modt[:, 0, :], "G1": modt[:, 1, :], "GA1": modt[:, 2, :],
                    "S2": modt[:, 3, :], "G2": modt[:, 4, :], "GA2": modt[:, 5, :]}

        class NormCtx:
            def __init__(self, es, tag, nbuf=2, npt=2, nt32=None):
                self.nbuf = nbuf
                self.npt = npt
                self.nt32 = nbuf if nt32 is None else nt32
                self.junk = mk.sb(es, tag + "junk", [128, D], F32)
                self.t32 = [mk.sb(es, tag + "t32_%d" % i, [128, D], F32) for i in range(self.nt32)]
                self.hb = [mk.sb(es, tag + "hb%d" % i, [128, D], BF16) for i in range(nbuf)]
                self.ss = [mk.sb(es, tag + "ss%d" % i, [128, 4], F32) for i in range(nbuf)]
                self.pt = [mk.ps(es, tag + "pt%d" % i, [128, 1024], BF16) for i in range(npt)]
                self.i = 0

        def rstd_of(xt, ss, junk, n):
            mk.memset("dve", ss[:, 0:1], 0.0)
            mk.act(junk, xt, AF.Square, accum=ss[:, 0:1])
            mk.ts("dve", ss[:, 1:2], ss[:, 0:1], 1.0 / n, EPS, ALU.mult, ALU.add)
            mk.act(ss[:, 2:3], ss[:, 1:2], AF.Ln)
            mk.act(ss[:, 3:4], ss[:, 2:3], AF.Exp, scale=-0.5)
            return ss[:, 3:4]

        def norm_tile(ctx, xt, G, S, dst):
            i = ctx.i
            ctx.i += 1
            ss, t32, hb, pt = ctx.ss[i % ctx.nbuf], ctx.t32[i % ctx.nt32], ctx.hb[i % ctx.nbuf], ctx.pt[i % ctx.npt]
            r = rstd_of(xt, ss, ctx.junk, D)
            mk.stt("dve", t32, xt, r, G, ALU.mult, ALU.mult)
            mk.tt("pool", hb, t32, S, ALU.add)
            for k in range(8):
                mk.transpose(pt[:, k * 128:(k + 1) * 128], hb[:, k * 128:(k + 1) * 128], identb)
            mk.copy("act", dst, pt.re("p (k t) -> p k t", k=8))

        def norm_A(ctx, xt, G, S):
            i = ctx.i
            ctx.i += 1
            ss, t32, hb = ctx.ss[i % ctx.nbuf], ctx.t32[i % ctx.nt32], ctx.hb[i % ctx.nbuf]
            r = rstd_of(xt, ss, ctx.junk, D)
            mk.stt("dve", t32, xt, r, G, ALU.mult, ALU.mult)
            mk.tt("pool", hb, t32, S, ALU.add)
            return i

        def norm_B(ctx, i, dst):
            hb, pt = ctx.hb[i % ctx.nbuf], ctx.pt[i % ctx.npt]
            for k in range(8):
                mk.transpose(pt[:, k * 128:(k + 1) * 128], hb[:, k * 128:(k + 1) * 128], identb)
            mk.copy("act", dst, pt.re("p (k t) -> p k t", k=8))

        def BL(t, axis, shape):
            return T(t.ap.unsqueeze(axis).to_broadcast(list(shape)), t.buf)

        def ROW(dt_, n, off=0):
            return T(bass.AP(dt_.ap.tensor, off, [[0, 128], [1, n]]), Buf())

        def U(t):
            return T(t.ap, Buf())

        def UA(t, off, steps):
            return T(bass.AP(t.ap.tensor, off, [list(x) for x in steps]), Buf())

        class PRot:
            def __init__(self, es, n, tag):
                self.p = [mk.ps(es, "%s%d" % (tag, i), [128, 512], F32) for i in range(n)]
                self.i = 0

            def __call__(self):
                p = self.p[self.i % len(self.p)]
                self.i += 1
                return p

        ecnt = [0]

        def evac_eng():
            ecnt[0] += 1
            return "act" if ecnt[0] % 2 else "dve"

        def dbg_out(src_sb, ncols):
            mk.dma(U(DBG)[:, 0:ncols], src_sb)

        HTF = dscr("HTF", [D, TO], BF16)
        XMID = dscr("XMID", [TO, D], F32)
        UC = dscr("UC", [D, 2048], F32)

        def ffn(l, Xin, Xout, ntiles, mod, final):
            TG = 4
            wg_l = T(w_gate.ap[l], w_gate.buf).re("(k p) n -> p k n", p=128)
            wu_l = T(w_up.ap[l], w_up.buf).re("(k p) n -> p k n", p=128)
            wd_l = T(w_down.ap[l], w_down.buf).re("(k p) n -> p k n", p=128)
            HTFv = U(HTF).re("(k p) t -> p k t", p=128)
            for ps_ in range(2):
                f0 = ps_ * 11
                with ExitStack() as es:
                    Wg = mk.sb(es, "Wg", [128, 8, 1408], BF16)
                    Wu = mk.sb(es, "Wu", [128, 8, 1408], BF16)
                    Wd = mk.sb(es, "Wd", [128, 11, D], BF16)
                    for i in range(2):
                        mk.dma(Wg[:, :, i * 704:(i + 1) * 704], wg_l[:, :, f0 * 128 + i * 704:f0 * 128 + (i + 1) * 704], q="pool")
                        mk.dma(Wu[:, :, i * 704:(i + 1) * 704], wu_l[:, :, f0 * 128 + i * 704:f0 * 128 + (i + 1) * 704], q="pool")
                    mk.dma(Wd[:, 0:6, :], wd_l[:, f0:f0 + 6, :], q="pool")
                    mk.dma(Wd[:, 6:11, :], wd_l[:, f0 + 6:f0 + 11, :], q="pool")
                    nctx = NormCtx(es, "nf", 4, 2, 2) if ps_ == 0 else None
                    nids = {}
                    pf = PRot(es, 6, "pff")
                    xg = [mk.sb(es, "xg%d" % i, [128, TG, D], F32) for i in range(2)]
                    hTf = [mk.sb(es, "hTf%d" % i, [128, 8, TG * 128], BF16) for i in range(2)]
                    hid = [mk.sb(es, "hid%d" % i, [128, 11, TG * 128], BF16) for i in range(2)]
                    sgl = [mk.sb(es, "sgl%d" % i, [128, TG * 128], F32) for i in range(2)]
                    tmpf = [mk.sb(es, "tmpf%d" % i, [128, 512], F32) for i in range(2)]
                    xnf = [mk.sb(es, "xnf%d" % i, [128, D], F32) for i in range(2)]
                    if final and ps_ == 1:
                        fng_b = mk.sb(es, "fng_b", [128, D], F32)
                        mk.dma(fng_b, ROW(fng, D))
                        ssf = mk.sb(es, "ssf", [128, 4], F32)
                        junkf = mk.sb(es, "junkf", [128, D], F32)
                        yo = [mk.sb(es, "yo%d" % i, [128, D], F32) for i in range(2)]
                    src = Xin if ps_ == 0 else XMID
                    groups = [(g0, min(TG, ntiles - g0)) for g0 in range(0, ntiles, TG)]

                    def f_NA(gi):
                        g0, nt = groups[gi]
                        xg_ = xg[gi % 2]
                        for i in range(nt):
                            mk.dma(xg_[:, i, :], U(src)[(g0 + i) * 128:(g0 + i + 1) * 128, :], par=(i > 0))

                    def f_NA2(gi):
                        g0, nt = groups[gi]
                        xg_ = xg[gi % 2]
                        if ps_ == 0:
                            nids[gi] = [norm_A(nctx, xg_[:, i, :], mod["G2"], mod["S2"]) for i in range(nt)]

                    def f_NB(gi):
                        g0, nt = groups[gi]
                        N = nt * 128
                        hT_ = hTf[gi % 2]
                        if ps_ == 0:
                            for i in range(nt):
                                norm_B(nctx, nids[gi][i], hT_[:, :, i * 128:(i + 1) * 128])
                            mk.dma(HTFv[:, :, g0 * 128:g0 * 128 + N], hT_[:, :, 0:N])
                        else:
                            mk.dma(hT_[:, :, 0:N], HTFv[:, :, g0 * 128:g0 * 128 + N])

                    def f_GU(gi):
                        g0, nt = groups[gi]
                        N = nt * 128
                        hT_ = hTf[gi % 2]
                        hid_ = hid[gi % 2]
                        for ft in range(11):
                            if ft == 5 and gi + 1 < len(groups):
                                f_NA2(gi + 1)
                            pg = pf()
                            pu = pf()
                            for k in range(8):
                                mk.mm(pg[:, 0:N], Wg[:, k, ft * 128:(ft + 1) * 128], hT_[:, k, 0:N], start=(k == 0), stop=(k == 7))
                            for k in range(8):
                                mk.mm(pu[:, 0:N], Wu[:, k, ft * 128:(ft + 1) * 128], hT_[:, k, 0:N], start=(k == 0), stop=(k == 7))
                            sg_ = sgl[ft % 2]
                            mk.act(sg_[:, 0:N], pg[:, 0:N], AF.Silu)
                            mk.tt("dve", hid_[:, ft, 0:N], sg_[:, 0:N], pu[:, 0:N], ALU.mult)

                    def f_DN(gi):
                        g0, nt = groups[gi]
                        xg_ = xg[gi % 2]
                        hid_ = hid[gi % 2]
                        for i in range(nt):
                            xn = xnf[i % 2]
                            for c2 in range(2):
                                p = pf()
                                for ft in range(11):
                                    mk.mm(p, hid_[:, ft, i * 128:(i + 1) * 128], Wd[:, ft, c2 * 512:(c2 + 1) * 512],
                                          start=(ft == 0), stop=(ft == 10))
                                t_ = tmpf[c2]
                                mk.tt("dve", t_, p, mod["GA2"][:, c2 * 512:(c2 + 1) * 512], ALU.mult)
                                mk.tt("pool", xn[:, c2 * 512:(c2 + 1) * 512], t_, xg_[:, i, c2 * 512:(c2 + 1) * 512], ALU.add)
                            row = slice((g0 + i) * 128, (g0 + i + 1) * 128)
                            if ps_ == 0:
                                mk.dma(U(XMID)[row, :], xn)
                            elif not final:
                                mk.dma(U(Xout)[row, :], xn)
                            else:
                                r = rstd_of(xn, ssf, junkf, D)
                                y_ = yo[i % 2]
                                mk.stt("dve", y_, xn, r, fng_b, ALU.mult, ALU.mult)
                                mk.dma(yout[row, :], y_)

                    f_NA(0)
                    f_NA2(0)
                    f_NB(0)
                    for gi in range(len(groups)):
                        if gi + 1 < len(groups):
                            f_NA(gi + 1)
                        f_GU(gi)
                        if gi + 1 < len(groups):
                            f_NB(gi + 1)
                        f_DN(gi)
                    mk.barrier()

        def conformer(mod):
            with ExitStack() as es:
                hT1 = mk.sb(es, "hT1", [128, 8, TO], BF16)
                with ExitStack() as es2:
                    nctx = NormCtx(es2, "n1", 4, 3)
                    xt = [mk.sb(es2, "xt1_%d" % i, [128, D], F32) for i in range(4)]
                    def ncA(j):
                        x_ = xt[j % 4]
                        mk.dma(x_, U(X2)[j * 128:(j + 1) * 128, :])
                        return norm_A(nctx, x_, mod["G1"], mod["S1"])

                    ids = {0: ncA(0)}
                    for j in range(NOT):
                        if j + 1 < NOT:
                            ids[j + 1] = ncA(j + 1)
                        norm_B(nctx, ids[j], hT1[:, :, j * 128:(j + 1) * 128])
                    mk.barrier()
                pf = PRot(es, 6, "pfc")
                W1 = mk.sb(es, "W1", [128, 8, 2048], BF16)
                for i in range(4):
                    wload(W1[:, :, i * 512:(i + 1) * 512], w_pw1, i * 512, 512)
                b1 = mk.sb(es, "b1", [128, 16], F32)
                wdw = mk.sb(es, "wdw", [128, 8 * 31], F32)
                bdw = mk.sb(es, "bdw", [128, 8], F32)
                mk.dma(b1, b_pw1)
                mk.dma(wdw, w_dw)
                mk.dma(bdw, b_dw)
                ubf = [mk.sb(es, "ubf%d" % i, [128, 30 + 2048], BF16) for i in range(2)]
                dgl = [mk.sb(es, "dg%d" % i, [128, 31, 128], BF16) for i in range(2)]
                ucs = [mk.sb(es, "ucs%d" % i, [128, 2048], F32) for i in range(2)]
                sgm = [mk.sb(es, "sgm%d" % i, [128, 512], F32) for i in range(2)]
                t30 = mk.sb(es, "t30", [128, 32], F32)
                pieces = [(96, 32)] + [(128 + i * 512, 512) for i in range(4)]

                def cA(ct):
                    u_ = ubf[ct % 2]
                    for pi, (u0, n) in enumerate(pieces):
                        pa = pf()
                        pg = pf()
                        for k in range(8):
                            mk.mm(pa[:, 0:n], W1[:, k, ct * 128:(ct + 1) * 128], hT1[:, k, u0:u0 + n], start=(k == 0), stop=(k == 7))
                        for k in range(8):
                            mk.mm(pg[:, 0:n], W1[:, k, 1024 + ct * 128:1024 + (ct + 1) * 128], hT1[:, k, u0:u0 + n],
                                  start=(k == 0), stop=(k == 7))
                        sg_ = sgm[pi % 2]
                        mk.act(sg_[:, 0:n], pg[:, 0:n], AF.Sigmoid, bias=b1[:, 8 + ct:9 + ct])
                        if pi == 0:
                            mk.stt("dve", t30, pa[:, 0:32], b1[:, ct:ct + 1], sg_[:, 0:32], ALU.add, ALU.mult)
                            mk.act(u_[:, 0:30], t30[:, 2:32], AF.Identity, scale=flag)
                        else:
                            o = 30 + u0 - 128
                            mk.stt("dve", u_[:, o:o + n], pa[:, 0:n], b1[:, ct:ct + 1], sg_[:, 0:n], ALU.add, ALU.mult)
                    dg = dgl[ct % 2]
                    for k in range(31):
                        mk.act(dg[:, k, :], identb, AF.Identity, scale=wdw[:, ct * 31 + k:ct * 31 + k + 1])

                def cB(ct):
                    u_ = ubf[ct % 2]
                    dg = dgl[ct % 2]
                    uc_ = ucs[ct % 2]
                    for t4 in range(4):
                        p = pf()
                        for k in range(31):
                            mk.mm(p, dg[:, k, :], u_[:, k + t4 * 512:k + t4 * 512 + 512], start=(k == 0), stop=(k == 30))
                        mk.act(uc_[:, t4 * 512:(t4 + 1) * 512], p, AF.Identity, bias=bdw[:, ct:ct + 1])
                    mk.dma(U(UC)[ct * 128:(ct + 1) * 128, :], uc_)

                cA(0)
                for ct in range(8):
                    if ct + 1 < 8:
                        cA(ct + 1)
                    cB(ct)
                mk.barrier()
            with ExitStack() as es:
                W2 = mk.sb(es, "W2", [128, 8, D], BF16)
                wload(W2[:, :, 0:512], w_pw2, 0, 512)
                wload(W2[:, :, 512:1024], w_pw2, 512, 512)
                b2_b = mk.sb(es, "b2_b", [128, D], F32)
                mk.dma(b2_b, ROW(b_pw2, D))
                lng = mk.sb(es, "lng", [128, 8], F32)
                lnb = mk.sb(es, "lnb", [128, 8], F32)
                mk.dma(lng, ln_g)
                mk.dma(lnb, ln_b)
                ucg = [mk.sb(es, "ucg%d" % i, [128, 8, 512], F32) for i in range(2)]
                sq = mk.sb(es, "sq", [128, 8, 512], F32)
                mean_l = [mk.sb(es, "mean%d" % i, [128, 512], F32) for i in range(2)]
                ex2_l = [mk.sb(es, "ex2%d" % i, [128, 512], F32) for i in range(2)]
                rs_l = [mk.sb(es, "rs%d" % i, [128, 512], F32) for i in range(2)]
                t1 = [mk.sb(es, "t1_%d" % i, [128, 512], F32) for i in range(2)]
                vfm_l = [mk.sb(es, "vfm%d" % i, [128, 8, 512], BF16) for i in range(2)]
                x2t = [mk.sb(es, "x2t%d" % i, [128, D], F32) for i in range(2)]
                xn1 = [mk.sb(es, "xn1_%d" % i, [128, D], F32) for i in range(2)]
                tmpc = [mk.sb(es, "tmpc%d" % i, [128, 512], F32) for i in range(2)]
                p_m = mk.ps(es, "p_m", [128, 512], F32)
                p_q = mk.ps(es, "p_q", [128, 512], F32)
                pf = PRot(es, 4, "pfc2")
                UCv = U(UC).re("(c p) t -> p c t", p=128)

                def cL(tg):
                    uc_ = ucg[tg % 2]
                    mean, ex2, rs, vfm = mean_l[tg % 2], ex2_l[tg % 2], rs_l[tg % 2], vfm_l[tg % 2]
                    mk.dma(uc_, UCv[:, :, tg * 512:(tg + 1) * 512])
                    mk.act(sq, uc_, AF.Square)
                    for ct in range(8):
                        mk.mm(p_m, ones, uc_[:, ct, :], start=(ct == 0), stop=(ct == 7))
                    for ct in range(8):
                        mk.mm(p_q, ones, sq[:, ct, :], start=(ct == 0), stop=(ct == 7))
                    mk.act(mean, p_m, AF.Identity, scale=1.0 / D)
                    mk.act(ex2, p_q, AF.Identity, scale=1.0 / D)
                    mk.tt("dve", rs, mean, mean, ALU.mult)
                    mk.tt("dve", rs, ex2, rs, ALU.subtract)
                    mk.ts("dve", rs, rs, 1.0, EPS, ALU.mult, ALU.add)
                    mk.act(rs, rs, AF.Ln)
                    mk.act(rs, rs, AF.Exp, scale=-0.5)
                    for ct in range(8):
                        t_ = t1[ct % 2]
                        mk.tt("dve", t_, uc_[:, ct, :], mean, ALU.subtract)
                        mk.tt("pool", t_, t_, rs, ALU.mult)
                        mk.act(vfm[:, ct, :], t_, AF.Silu, scale=lng[:, ct:ct + 1], bias=lnb[:, ct:ct + 1])

                def cM(tg):
                    vfm = vfm_l[tg % 2]
                    for i in range(4):
                        ti = tg * 4 + i
                        x_ = x2t[i % 2]
                        mk.dma(x_, U(X2)[(1 + ti) * 128:(2 + ti) * 128, :])
                        xn = xn1[i % 2]
                        for c2 in range(2):
                            p = pf()
                            for ct in range(8):
                                mk.mm(p, vfm[:, ct, i * 128:(i + 1) * 128], W2[:, ct, c2 * 512:(c2 + 1) * 512],
                                      start=(ct == 0), stop=(ct == 7))
                            t_ = tmpc[c2]
                            mk.tt("dve", t_, p, b2_b[:, c2 * 512:(c2 + 1) * 512], ALU.add)
                            mk.tt("dve", t_, t_, mod["GA1"][:, c2 * 512:(c2 + 1) * 512], ALU.mult)
                            mk.tt("pool", xn[:, c2 * 512:(c2 + 1) * 512], t_, x_[:, c2 * 512:(c2 + 1) * 512], ALU.add)
                        mk.dma(U(X3)[ti * 128:(ti + 1) * 128, :], xn, q="pool")

                cL(0)
                for tg in range(4):
                    if tg + 1 < 4:
                        cL(tg + 1)
                    cM(tg)
                mk.barrier()

        es_l0 = ExitStack()
        with es_l0:
            mod0 = compute_mod(es_l0, 0)
            es_dt = ExitStack()
            es_dt.__enter__()
            dtall = mk.sb(es_dt, "dtall", [128, 32, 32], F32)
            dtAall = mk.sb(es_dt, "dtAall", [128, 32, 32], F32)
            if upto >= 1:
                es_s12 = ExitStack()
                with es_s12:
                    hT = mk.sb(es_s12, "hT", [128, 8, TE], BF16)
                    es_z = ExitStack()
                    es_z.__enter__()
                    Wz = mk.sb(es_z, "Wz", [128, 8, 2048], BF16)
                    for i in range(4):
                        wload(Wz[:, :, i * 512:(i + 1) * 512], w_in, i * 512, 512)
                    Wdt = mk.sb(es_z, "Wdt", [128, 8, 32], BF16)
                    wload(Wdt, w_in, 5120, 32)
                    with ExitStack() as es:
                        nctx = NormCtx(es, "n0", 4, 3)
                        xt = [mk.sb(es, "xt%d" % i, [128, D], F32) for i in range(4)]
                        def n1A(t):
                            x_ = xt[t % 4]
                            mk.dma(x_, xe[t * 128:(t + 1) * 128, :])
                            return norm_A(nctx, x_, mod0["G1"], mod0["S1"])

                        ids = {0: n1A(0)}
                        for t in range(32):
                            if t + 1 < 32:
                                ids[t + 1] = n1A(t + 1)
                            norm_B(nctx, ids[t], hT[:, :, t * 128:(t + 1) * 128])
                        mk.barrier()
                    if upto >= 2:
                        if True:
                            with ExitStack() as es:
                                pf = PRot(es, 6, "pf2b")
                                zst = [mk.sb(es, "zst%d" % i, [128, 2048], F32) for i in range(2)]
                                dtb_b = mk.sb(es, "dtb_b", [128, 32], F32)
                                A_b = mk.sb(es, "A_b", [128, 32], F32)
                                mk.dma(dtb_b, ROW(dtb, 32))
                                mk.dma(A_b, ROW(alog, 32))
                                mk.act(A_b, A_b, AF.Exp)
                                mk.ts("dve", A_b, A_b, -1.0, 0.0, ALU.mult, ALU.add)
                                for half in range(2):
                                    p = pf()
                                    for tl in range(16):
                                        t = half * 16 + tl
                                        for k in range(8):
                                            mk.mm(p[:, tl * 32:(tl + 1) * 32], hT[:, k, t * 128:(t + 1) * 128], Wdt[:, k, :],
                                                  start=(k == 0), stop=(k == 7))
                                    tmp = dtall[:, half * 16:(half + 1) * 16, :]
                                    mk.tt("dve", tmp, p.re("p (t h) -> p t h", t=16), BL(dtb_b, 1, [128, 16, 32]), ALU.add)
                                    mk.act(tmp, tmp, AF.Exp)
                                    mk.act(tmp, tmp, AF.Ln, bias=1.0)
                                    mk.tt("dve", dtAall[:, half * 16:(half + 1) * 16, :], tmp, BL(A_b, 1, [128, 16, 32]), ALU.mult)
                                for j in range(NOT):
                                    t = OT0 + j
                                    zs = zst[j % 2]
                                    for c4 in range(4):
                                        p = pf()
                                        for k in range(8):
                                            mk.mm(p, hT[:, k, t * 128:(t + 1) * 128], Wz[:, k, c4 * 512:(c4 + 1) * 512],
                                                  start=(k == 0), stop=(k == 7))
                                        mk.act(zs[:, c4 * 512:(c4 + 1) * 512], p, AF.Silu)
                                    mk.dma(U(ZS)[j * 128:(j + 1) * 128, :], zs)
                                mk.barrier()
                        es_z.close()
                        with ExitStack() as es:
                            pf = PRot(es, 5, "pf2a")
                            ptb = [mk.ps(es, "ptb%d" % i, [128, 1024], BF16) for i in range(2)]
                            cw = mk.sb(es, "cw", [128, 96], F32)
                            cbv = mk.sb(es, "cbv", [128, 24], F32)
                            mk.dma(cw, convw)
                            mk.dma(cbv, convb)
                            Wc = [mk.sb(es, "Wc%d" % i, [128, 8, 512], BF16) for i in range(2)]
                            raw = [mk.sb(es, "raw%d" % i, [128, 3 + TE], F32) for i in range(2)]
                            acc = [mk.sb(es, "acc%d" % i, [128, 2048], F32) for i in range(2)]
                            xc = [mk.sb(es, "xc%d" % i, [128, TE], BF16) for i in range(2)]
                            stg = [mk.sb(es, "stg%d" % i, [128, 32, 128], BF16) for i in range(2)]
                            for r_ in raw:
                                mk.memset("pool", r_[:, 0:3], 0.0)
                            XSBv = U(XSB).re("(t p) c -> p t c", p=128)
                            def s2a_A(ct):
                                cbk, ci = divmod(ct, 4)
                                W = Wc[cbk % 2]
                                if ci == 0:
                                    wload(W, w_in, 2048 + cbk * 512, 512)
                                rw = raw[ct % 2]
                                for tt in range(8):
                                    p = pf()
                                    for k in range(8):
                                        mk.mm(p, W[:, k, ci * 128:(ci + 1) * 128], hT[:, k, tt * 512:(tt + 1) * 512],
                                              start=(k == 0), stop=(k == 7))
                                    dst = rw[:, 3 + tt * 512:3 + (tt + 1) * 512]
                                    if tt < 4:
                                        mk.act(dst, p, AF.Identity, scale=flag)
                                    else:
                                        mk.copy("act", dst, p)

                            def s2a_B(ct):
                                rw = raw[ct % 2]
                                x_c = xc[ct % 2]
                                for hv in range(2):
                                    o = hv * 2048
                                    a_ = acc[hv]
                                    mk.ts("dve", a_, rw[:, 3 + o:3 + o + 2048], cw[:, ct * 4 + 3:ct * 4 + 4],
                                          cbv[:, ct:ct + 1], ALU.mult, ALU.add)
                                    for kk in (2, 1, 0):
                                        mk.stt("dve", a_, rw[:, kk + o:kk + o + 2048], cw[:, ct * 4 + kk:ct * 4 + kk + 1],
                                               a_, ALU.mult, ALU.add)
                                    mk.act(x_c[:, o:o + 2048], a_, AF.Silu)

                            def s2a_C(ct):
                                x_c = xc[ct % 2]
                                if ct < 20:
                                    sg = stg[ct % 2]
                                    for tb in range(4):
                                        pt = ptb[tb % 2]
                                        for j in range(8):
                                            mk.transpose(pt[:, j * 128:(j + 1) * 128],
                                                         x_c[:, (tb * 8 + j) * 128:(tb * 8 + j + 1) * 128], identb)
                                        mk.copy("act", sg[:, tb * 8:(tb + 1) * 8, :], pt.re("p (j c) -> p j c", j=8))
                                    for q4 in range(4):
                                        mk.dma(XSBv[:, q4 * 8:(q4 + 1) * 8, ct * 128:(ct + 1) * 128], sg[:, q4 * 8:(q4 + 1) * 8, :])
                                if ct >= 16:
                                    mk.dma(U(BCF)[(ct - 16) * 128:(ct - 15) * 128, :], x_c)

                            s2a_A(0)
                            for ct in range(24):
                                if ct + 1 < 24:
                                    s2a_A(ct + 1)
                                s2a_B(ct)
                                s2a_C(ct)
                            mk.barrier()
                        if upto >= 2.6:
                            with ExitStack() as es:
                                pf = PRot(es, 6, "pf2d")
                                Wq = [mk.sb(es, "Wq%d" % i, [128, 8, 512], BF16) for i in range(2)]
                                qst = [mk.sb(es, "qst%d" % i, [128, TO], BF16) for i in range(2)]
                                kst = [[mk.sb(es, "kst%d_%d" % (i, g), [128, TE], BF16) for g in range(3)] for i in range(2)]
                                pieces = [(0, 128)] + [(128 + i * 512, 512) for i in range(4)]
                                wi = 0
                                import os
                                SK = os.environ.get('K_SKIP', '')
                                for g, d in enumerate(DILS if 'q' not in SK else ()):
                                    for fb in range(2):
                                        W = Wq[wi % 2]
                                        wi += 1
                                        wload(W, w_in, 5152 + g * 1024 + fb * 512, 512)
                                        for fi in range(4):
                                            ft = fb * 4 + fi
                                            qs = qst[ft % 2]
                                            for (u0, n) in pieces:
                                                p = pf()
                                                for k in range(8):
                                                    mk.mm(p[:, 0:n], W[:, k, fi * 128:(fi + 1) * 128], hT[:, k, U0 + u0:U0 + u0 + n],
                                                          start=(k == 0), stop=(k == 7))
                                                mk.copy(evac_eng(), qs[:, u0:u0 + n], p[:, 0:n])
                                            mk.dma(U(Qd)[g, ft * 128:(ft + 1) * 128, 0:1024], qs[:, 0:1024])
                                            mk.dma(U(Qd)[g, ft * 128:(ft + 1) * 128, 1024:TO], qs[:, 1024:TO])
                                for fb in range(2 if 'k' not in SK else 0):
                                    W = Wq[wi % 2]
                                    wi += 1
                                    wload(W, w_in, 8224 + fb * 512, 512)
                                    for fi in range(4):
                                        ft = fb * 4 + fi
                                        for tt in range(8):
                                            p = pf()
                                            for k in range(8):
                                                mk.mm(p, W[:, k, fi * 128:(fi + 1) * 128], hT[:, k, tt * 512:(tt + 1) * 512],
                                                      start=(k == 0), stop=(k == 7))
                                            mk.copy(evac_eng(), kst[ft % 2][0][:, tt * 512:(tt + 1) * 512], p)
                                        mk.dma(U(Kd)[0, ft * 128:(ft + 1) * 128, :], kst[ft % 2][0])
                                Wv = mk.sb(es, "Wv", [128, 8, 1024], BF16)
                                wload(Wv[:, :, 0:512], w_in, 9248, 512)
                                wload(Wv[:, :, 512:1024], w_in, 9248 + 512, 512)
                                vst = [mk.sb(es, "vst%d" % i, [128, 16, 66], BF16) for i in range(2)]
                                for v_ in vst:
                                    mk.memset("pool", v_, 0.0)
                                for t in range(32 if 'v' not in SK else 0):
                                    vs = vst[t % 2]
                                    for h2 in range(2):
                                        p = pf()
                                        for k in range(8):
                                            mk.mm(p, hT[:, k, t * 128:(t + 1) * 128], Wv[:, k, h2 * 512:(h2 + 1) * 512],
                                                  start=(k == 0), stop=(k == 7))
                                        dst = vs[:, h2 * 8:(h2 + 1) * 8, 0:64]
                                        src = p.re("p (h d) -> p h d", h=8)
                                        if t < 16:
                                            mk.act(dst, src, AF.Identity, scale=flag)
                                        else:
                                            mk.copy(evac_eng(), dst, src)
                                    if t < 16:
                                        mk.copy("dve", vs[:, :, 64:65], BL(flag, 1, [128, 16, 1]))
                                    else:
                                        mk.memset("dve", vs[:, :, 64:65], 1.0)
                                    mk.dma(U(VA)[t * 128:(t + 1) * 128, :], vs.re("p h d -> p (h d)"))
                                mk.barrier()
            if upto >= 3:
                with ExitStack() as es:
                    p_acs = mk.ps(es, "p_acs", [128, 512], F32)
                    p_cb = mk.ps(es, "p_cb", [128, 512], F32)
                    p_seg = [mk.ps(es, "p_seg%d" % i, [128, 512], F32) for i in range(2)]
                    p_yd = mk.ps(es, "p_yd", [128, 512], F32)
                    p_yo = mk.ps(es, "p_yo", [128, 512], F32)
                    p_st = mk.ps(es, "p_st", [128, 512], F32)
                    p_t = mk.ps(es, "p_t", [128, 1024], BF16)
                    H = mk.sb(es, "H", [128, 2048], F32)
                    Hb = mk.sb(es, "Hb", [128, 2048], BF16)
                    mk.memset("dve", H, 0.0)
                    mk.memset("pool", Hb, 0.0)
                    dsk_b = mk.sb(es, "dsk_b", [128, 32], F32)
                    ssmg_b = mk.sb(es, "ssmg_b", [128, 2048], F32)
                    mk.dma(dsk_b, ROW(dskip, 32))
                    mk.dma(ssmg_b, ROW(ssmg, 2048))
                    xsb = [mk.sb(es, "xsb%d" % i, [128, 2560], BF16) for i in range(3)]
                    bfm = [mk.sb(es, "bfm%d" % i, [128, 4, 128], BF16) for i in range(2)]
                    cfm = [mk.sb(es, "cfm%d" % i, [128, 4, 128], BF16) for i in range(2)]
                    ztl = [mk.sb(es, "ztl%d" % i, [128, 2048], F32) for i in range(3)]
                    acs_l = [mk.sb(es, "acs_sb%d" % i, [128, 64], F32) for i in range(2)]
                    dend_l = [mk.sb(es, "dend%d" % i, [128, 32], F32) for i in range(2)]
                    wgt_l = [mk.sb(es, "wgt%d" % i, [128, 32], F32) for i in range(2)]
                    eacs_l = [mk.sb(es, "eacs%d" % i, [128, 32], F32) for i in range(2)]
                    cdec_l = [mk.sb(es, "cdec%d" % i, [128, 32], F32) for i in range(2)]
                    xw_l = [mk.sb(es, "xw%d" % i, [128, 2048], BF16) for i in range(2)]
                    xdt_l = [mk.sb(es, "xdt%d" % i, [128, 2048], BF16) for i in range(2)]
                    CBm_l = [mk.sb(es, "CBm%d" % i, [128, 512], F32) for i in range(2)]
                    Dm_l = [mk.sb(es, "Dm%d" % i, [128, 4, 128], F32) for i in range(2)]
                    Lt_l = [mk.sb(es, "Lt%d" % i, [128, 512], F32) for i in range(2)]
                    Mt_l = [mk.sb(es, "Mt%d" % i, [128, 4, 128], BF16) for i in range(3)]
                    yacc_l = [mk.sb(es, "yacc%d" % i, [128, 2048], F32) for i in range(2)]
                    tmpg_l = [mk.sb(es, "tmpg%d" % i, [128, 512], F32) for i in range(2)]
                    tmp2 = mk.sb(es, "tmp2", [128, 2048], F32)
                    junk2 = mk.sb(es, "junk2", [128, 2048], F32)
                    ss3 = mk.sb(es, "ss3", [128, 4], F32)
                    ynb = mk.sb(es, "ynb", [128, 2048], BF16)
                    ynT = [mk.sb(es, "ynT%d" % i, [128, 16, 128], BF16) for i in range(2)]
                    BCFv = U(BCF)
                    YNTv = U(YNT).re("(k p) t -> p k t", p=128)

                    def xs3_of(c):
                        return xsb[c % 3][:, 0:2048].re("p (h d) -> p h d", h=32)

                    def P1(c):
                        own = c >= OT0
                        xs_ = xsb[c % 3]
                        mk.dma(xs_, U(XSB)[c * 128:(c + 1) * 128, :])
                        bf_ = bfm[c % 2]
                        mk.dma(bf_, BCFv[0:512, c * 128:(c + 1) * 128].re("(g n) s -> n g s", g=4))
                        if own:
                            cf_ = cfm[c % 2]
                            mk.dma(cf_, BCFv[512:1024, c * 128:(c + 1) * 128].re("(g n) s -> n g s", g=4))
                            mk.dma(ztl[c % 3], U(ZS)[(c - OT0) * 128:(c - OT0 + 1) * 128, :])
                        dtA_c = dtAall[:, c, :]
                        dt_c = dtall[:, c, :]
                        acs_sb, dend, wgt, cdec = acs_l[c % 2], dend_l[c % 2], wgt_l[c % 2], cdec_l[c % 2]
                        mk.mm(p_acs[:, 0:32], tri, dtA_c)
                        mk.mm(p_acs[:, 32:64], ones, dtA_c)
                        mk.copy("act", acs_sb, p_acs[:, 0:64])
                        mk.tt("dve", dend, acs_sb[:, 32:64], acs_sb[:, 0:32], ALU.subtract)
                        mk.act(dend, dend, AF.Exp)
                        mk.act(cdec, acs_sb[:, 32:64], AF.Exp)
                        mk.tt("dve", wgt, dt_c, dend, ALU.mult)
                        mk.tt("dve", xw_l[c % 2].re("p (h d) -> p h d", h=32), xs3_of(c), BL(wgt, 2, [128, 32, 64]), ALU.mult)
                        if own:
                            mk.act(eacs_l[c % 2], acs_sb[:, 0:32], AF.Exp)
                            mk.tt("dve", xdt_l[c % 2].re("p (h d) -> p h d", h=32), xs3_of(c), BL(dt_c, 2, [128, 32, 64]), ALU.mult)
                            for g in range(4):
                                mk.mm(p_cb[:, g * 128:(g + 1) * 128], bf_[:, g, :], cf_[:, g, :])
                            mk.tt("dve", CBm_l[c % 2].re("p (g l) -> p g l", g=4), p_cb.re("p (g l) -> p g l", g=4),
                                  BL(tri, 1, [128, 4, 128]), ALU.mult)

                    def P2(c, u):
                        g = u // 2
                        dtA_c = dtAall[:, c, :]
                        Dm, Lt, Mt, ps_ = Dm_l[u % 2], Lt_l[u % 2], Mt_l[u % 3], p_seg[u % 2]
                        mk.tt("pool", Dm, BL(tri, 1, [128, 4, 128]), BL(dtA_c[:, u * 4:(u + 1) * 4], 2, [128, 4, 128]), ALU.mult)
                        mk.mm(ps_, ustr, Dm.re("p h l -> p (h l)"))
                        mk.act(Lt, ps_, AF.Exp)
                        mk.tt("dve", Mt, Lt.re("p (h l) -> p h l", h=4),
                              BL(CBm_l[c % 2][:, g * 128:(g + 1) * 128], 1, [128, 4, 128]), ALU.mult)

                    def P3(c, u):
                        g, hf = divmod(u, 2)
                        Mt = Mt_l[u % 3]
                        xdt = xdt_l[c % 2]
                        for hh in range(4):
                            h = u * 4 + hh
                            mk.mm(p_yd[:, (hf * 4 + hh) * 64:(hf * 4 + hh + 1) * 64], Mt[:, hh, :], xdt[:, h * 64:(h + 1) * 64])
                        if hf == 1:
                            tmpg = tmpg_l[g % 2]
                            mk.mm(p_yo, cfm[c % 2][:, g, :], Hb[:, g * 512:(g + 1) * 512])
                            mk.tt("dve", tmpg.re("p (h d) -> p h d", h=8), p_yo.re("p (h d) -> p h d", h=8),
                                  BL(eacs_l[c % 2][:, g * 8:(g + 1) * 8], 2, [128, 8, 64]), ALU.mult)
                            mk.tt("dve", yacc_l[c % 2][:, g * 512:(g + 1) * 512], tmpg, p_yd, ALU.add)

                    def P4(c):
                        xs_ = xsb[c % 3]
                        xw = xw_l[c % 2]
                        cdec = cdec_l[c % 2]
                        H3 = H.re("p (h d) -> p h d", h=32)
                        mk.tt("dve", H3, H3, BL(cdec, 2, [128, 32, 64]), ALU.mult)
                        pst = [p_st, p_yo]
                        def st_mm(g):
                            mk.mm(pst[g % 2], xs_[:, 2048 + g * 128:2048 + (g + 1) * 128], xw[:, g * 512:(g + 1) * 512])

                        st_mm(0)
                        st_mm(1)
                        for g in range(4):
                            Hg = H[:, g * 512:(g + 1) * 512]
                            mk.tt("dve", Hg, Hg, pst[g % 2], ALU.add)
                            if g + 2 < 4:
                                st_mm(g + 2)
                        if c == 15:
                            mk.act(H, H, AF.Identity, scale=flag)
                        mk.copy("act", Hb, H)

                    def P5a(c):
                        yacc = yacc_l[c % 2]
                        mk.tt("pool", tmp2.re("p (h d) -> p h d", h=32), xs3_of(c), BL(dsk_b, 2, [128, 32, 64]), ALU.mult)
                        mk.tt("dve", yacc, yacc, tmp2, ALU.add)
                        mk.tt("dve", yacc, yacc, ztl[c % 3], ALU.mult)
                        r = rstd_of(yacc, ss3, junk2, 2048)
                        mk.stt("dve", ynb, yacc, r, ssmg_b, ALU.mult, ALU.mult)

                    def P5b(c):
                        j = c - OT0
                        yT = ynT[c % 2]
                        for grp in range(2):
                            for k in range(8):
                                kk = grp * 8 + k
                                mk.transpose(p_t[:, k * 128:(k + 1) * 128], ynb[:, kk * 128:(kk + 1) * 128], identb)
                            mk.copy("act", yT[:, grp * 8:(grp + 1) * 8, :], p_t.re("p (k t) -> p k t", k=8))
                        mk.dma(YNTv[:, :, j * 128:(j + 1) * 128], yT)

                    P1(0)
                    for c in range(32):
                        own = c >= OT0
                        if c + 1 < 32 and not own:
                            P1(c + 1)
                        if own:
                            P2(c, 0)
                            P2(c, 1)
                            for u in range(8):
                                if u + 2 < 8:
                                    P2(c, u + 2)
                                P3(c, u)
                                if u == 3 and c + 1 < 32:
                                    P1(c + 1)
                                if c - 1 >= OT0:
                                    if u == 1:
                                        P5a(c - 1)
                                    if u == 5:
                                        P5b(c - 1)
                        if c < 31:
                            P4(c)
                    P5a(31)
                    P5b(31)
                    mk.barrier()
            es_dt.close()
            es_wo = ExitStack()
            es_wo.__enter__()
            Wo = mk.sb(es_wo, "Wo", [128, 24, D], BF16)
            wov = w_out.re("(k p) n -> p k n", p=128)
            for i in range(6):
                mk.dma(Wo[:, i * 4:(i + 1) * 4, :], wov[:, i * 4:(i + 1) * 4, :], q="pool")
            if upto >= 3.5:
                with ExitStack() as es:
                    relt_s = mk.sb(es, "relt_s", [32, 48], F32)
                    oh_s = mk.sb(es, "oh_s", [32, 3, 384], F32)
                    pm_s = mk.sb(es, "pm_s", [16, 3, 384], F32)
                    fu = mk.sb(es, "fu", [16, 3, 384], F32)
                    pp = mk.ps(es, "pp_e", [128, 512], F32)
                    pp2 = [mk.ps(es, "pp2_%d" % i, [128, 512], F32) for i in range(4)]
                    hk = [mk.sb(es, "hk%d" % i, [128, 256], F32) for i in range(8)]
                    est = mk.sb(es, "est", [128, 16, 256], F32)
                    mk.dma(relt_s, relt)
                    mk.dma(oh_s, onehot_d.re("g b x -> b g x"))
                    mk.dma(pm_s, pmask_d.re("g h x -> h g x"))
                    PMt = T(PM.ap, Buf())
                    for g in range(3):
                        mk.mm(pp[0:16, 0:384], relt_s[:, g * 16:(g + 1) * 16], oh_s[:, g, :])
                        mk.act(fu[:, g, :], pp[0:16, 0:384], AF.Exp)
                        mk.tt("dve", fu[:, g, :], fu[:, g, :], pm_s[:, g, :], ALU.mult)
                        mk.dma(PMt[g * 16:(g + 1) * 16, :], fu[:, g, :])
                    for g in range(3):
                        for h in range(16):
                            col = g * 16 + h
                            hk_ = hk[col % 8]
                            src = T(bass.AP(PM.ap.tensor, col * 384 + 1, [[1, 128], [128, 2], [1, 128]]), PMt.buf)
                            mk.dma(hk_.re("p (b i) -> p b i", b=2), src)
                            p2 = pp2[col % 4]
                            mk.mm(p2[:, 0:256], jrev, hk_)
                            mk.copy(evac_eng(), est[:, h, :], p2[:, 0:256])
                        mk.dma(U(ED)[g], est.re("p h x -> p (h x)"))
                    mk.barrier()
                if upto >= 4:
                    with ExitStack() as es:
                        VAg = mk.sb(es, "VAg", [128, 32, 1056], BF16)
                        VAb = [T(VAg.ap[:, kb, :], Buf("VAb%d" % kb)) for kb in range(32)]
                        Ep = [mk.sb(es, "Ep%d" % i, [128, 2, 256], F32) for i in range(3)]
                        Kp = [mk.sb(es, "Kp%d" % i, [128, TE], BF16) for i in range(2)]
                        Qp = [mk.sb(es, "Qp%d" % i, [128, TO], BF16) for i in range(2)]
                        p_s = [mk.ps(es, "p_s%d" % i, [128, 1024], F32) for i in range(3)]
                        p_o = [mk.ps(es, "p_o%d" % i, [128, 512], F32) for i in range(2)]
                        Ptl = [mk.sb(es, "Ptl%d" % i, [128, 512], F32) for i in range(3)]
                        Pbl = [mk.sb(es, "Pbl%d" % i, [128, 512], BF16) for i in range(3)]
                        ost = [mk.sb(es, "ost%d" % i, [128, 130], F32) for i in range(4)]
                        gcnt = [0]
                        for g, d in enumerate(DILS):
                            nb = 32 // d
                            mq0 = U0 // d
                            qblocks = []
                            for r in range(d):
                                m = mq0
                                while m < TE // d:
                                    mb = m // 128
                                    i0 = m % 128
                                    n = min(128 - i0, TE // d - m)
                                    qblocks.append((r, mb, i0, n))
                                    m += n
                            need = []
                            for (r, mb, i0, n) in qblocks:
                                kbc = r * nb + mb
                                for kb in ([kbc - 1] if mb >= 1 else []) + [kbc]:
                                    if kb not in need:
                                        need.append(kb)
                            for kb in need:
                                r, mb = divmod(kb, nb)
                                mk.dma(VAb[kb], UA(VA, (128 * mb * d + r) * 1056, [[d * 1056, 128], [1, 1056]]))
                            items = [(hp, qb) for hp in range(8) for qb in qblocks]
                            kq = {}

                            def load_pair(hp, g=g, kq=kq):
                                if hp in kq or hp >= 8:
                                    return
                                K_, Q_, E_ = Kp[hp % 2], Qp[hp % 2], Ep[hp % 3]
                                mk.dma(K_, U(Kd)[0, hp * 128:(hp + 1) * 128, :])
                                mk.dma(Q_[:, 0:1024], U(Qd)[g, hp * 128:(hp + 1) * 128, 0:1024])
                                mk.dma(Q_[:, 1024:TO], U(Qd)[g, hp * 128:(hp + 1) * 128, 1024:TO], par=True)
                                mk.dma(E_.re("p a x -> p (a x)"), U(ED)[g, :, hp * 512:(hp + 1) * 512])
                                kq[hp] = (K_, Q_, E_)

                            def a_qk(idx, g=g, d=d, nb=nb, mq0=mq0, items=items, kq=kq, load_pair=load_pair):
                                hp, (r, mb, i0, n) = items[idx]
                                load_pair(hp)
                                load_pair(hp + 1)
                                K_, Q_, E_ = kq[hp]
                                ps_ = p_s[idx % 3]
                                blks = [0, 1] if mb >= 1 else [0]
                                for hh in range(2):
                                    for blk in blks:
                                        c0 = hh * 512 + blk * 128
                                        ks = 128 * (mb - blk) * d + r
                                        qs0 = (128 * mb + i0 - mq0) * d + r
                                        mk.mm(ps_[:, c0:c0 + n], K_[hh * 64:(hh + 1) * 64, ks:ks + 127 * d + 1:d],
                                              Q_[hh * 64:(hh + 1) * 64, qs0:qs0 + (n - 1) * d + 1:d])

                            def a_sm(idx, items=items, kq=kq):
                                hp, (r, mb, i0, n) = items[idx]
                                E_ = kq[hp][2]
                                nbk = 2 if mb >= 1 else 1
                                ps_ = p_s[idx % 3]
                                Pt_ = Ptl[idx % 3]
                                Pb_ = Pbl[idx % 3]
                                v4 = lambda t_: t_.re("p (a b i) -> p a b i", a=2, b=2)[:, :, 0:nbk, 0:n]
                                mk.act(v4(Pt_), ps_.re("p (a b i) -> p a b i", a=2, b=4)[:, :, 0:nbk, 0:n], AF.Exp, scale=0.125)
                                Ev = E_.re("p a (b i) -> p a b i", b=2)[:, :, 0:nbk, i0:i0 + n]
                                mk.tt("dve", v4(Pb_), v4(Pt_), Ev, ALU.mult)

                            def a_pv(idx, g=g, d=d, nb=nb, mq0=mq0, items=items):
                                hp, (r, mb, i0, n) = items[idx]
                                kbc = r * nb + mb
                                blks = [0, 1] if mb >= 1 else [0]
                                nbk = len(blks)
                                Pb_ = Pbl[idx % 3]
                                po_ = p_o[idx % 2]
                                os_ = ost[gcnt[0] % 4]
                                gcnt[0] += 1
                                for hh in range(2):
                                    head = hp * 2 + hh
                                    for bi, blk in enumerate(blks):
                                        kb = kbc - blk
                                        c0 = (hh * 2 + blk) * 128
                                        mk.mm(po_[0:n, hh * 65:(hh + 1) * 65], Pb_[:, c0:c0 + n],
                                              VAb[kb][:, head * 66:head * 66 + 65], start=(bi == 0), stop=(bi == nbk - 1))
                                mk.copy("dve", os_[0:n, :], po_[0:n, 0:130])
                                u0 = (128 * mb + i0 - mq0) * d + r
                                mk.dma(UA(OA, g * TO * 1040 + u0 * 1040 + hp * 130, [[d * 1040, n], [1, 130]]), os_[0:n, :])

                            NI = len(items)
                            a_qk(0)
                            a_qk(1)
                            a_sm(0)
                            for idx in range(NI):
                                if idx + 2 < NI:
                                    a_qk(idx + 2)
                                if idx + 1 < NI:
                                    a_sm(idx + 1)
                                a_pv(idx)
                        mk.barrier()
            if upto >= 5:
                with ExitStack() as es:
                    oa = [[mk.sb(es, "oa%d_%d" % (i, g), [128, 1040], F32) for g in range(3)] for i in range(2)]
                    xt5 = [mk.sb(es, "xt5_%d" % i, [128, D], F32) for i in range(3)]
                    yT5 = [mk.sb(es, "yT5_%d" % i, [128, 16, 128], BF16) for i in range(2)]
                    attb = [mk.sb(es, "attb%d" % i, [128, D], BF16) for i in range(2)]
                    attT = [mk.sb(es, "attT%d" % i, [128, 8, 128], BF16) for i in range(2)]
                    rl = [mk.sb(es, "rl%d" % i, [128, 16], F32) for i in range(2)]
                    tmp5 = [mk.sb(es, "tmp5_%d" % i, [128, 512], F32) for i in range(2)]
                    xn5 = [mk.sb(es, "xn5_%d" % i, [128, D], F32) for i in range(2)]
                    p_t5 = [mk.ps(es, "p_t5_%d" % i, [128, 1024], BF16) for i in range(2)]
                    pf = PRot(es, 4, "pf5")
                    YNTv = U(YNT).re("(k p) t -> p k t", p=128)

                    def s5_prep(j):
                        o_ = oa[j % 2]
                        for g in range(3):
                            mk.dma(o_[g][:, 0:520], U(OA)[g, j * 128:(j + 1) * 128, 0:520])
                            mk.dma(o_[g][:, 520:1040], U(OA)[g, j * 128:(j + 1) * 128, 520:1040], par=True)
                        mk.dma(yT5[j % 2], YNTv[:, :, j * 128:(j + 1) * 128])
                        mk.dma(xt5[j % 3], xe[(OT0 + j) * 128:(OT0 + j + 1) * 128, :])
                        mk.tt("dve", o_[0], o_[0], o_[1], ALU.add)
                        mk.tt("dve", o_[0], o_[0], o_[2], ALU.add)
                        o3 = o_[0].re("p (h d) -> p h d", h=16)
                        rl_ = rl[j % 2]
                        ab_ = attb[j % 2]
                        mk.ts("dve", rl_, o3[:, :, 64], 1e-30, 1.0, ALU.max, ALU.mult)
                        mk.recip(rl_, rl_)
                        mk.tt("dve", ab_.re("p (h d) -> p h d", h=16), o3[:, :, 0:64], BL(rl_, 2, [128, 16, 64]), ALU.mult)
                        pt_ = p_t5[j % 2]
                        for k in range(8):
                            mk.transpose(pt_[:, k * 128:(k + 1) * 128], ab_[:, k * 128:(k + 1) * 128], identb)
                        mk.copy("act", attT[j % 2], pt_.re("p (k t) -> p k t", k=8))

                    def s5_main(j):
                        x_ = xt5[j % 3]
                        y_ = yT5[j % 2]
                        aT = attT[j % 2]
                        xn = xn5[j % 2]
                        for c2 in range(2):
                            p = pf()
                            for kc in range(24):
                                lhs = y_[:, kc, :] if kc < 16 else aT[:, kc - 16, :]
                                mk.mm(p, lhs, Wo[:, kc, c2 * 512:(c2 + 1) * 512], start=(kc == 0), stop=(kc == 23))
                            t5 = tmp5[c2]
                            mk.tt("dve", t5, p, mod0["GA1"][:, c2 * 512:(c2 + 1) * 512], ALU.mult)
                            mk.tt("pool", xn[:, c2 * 512:(c2 + 1) * 512], t5, x_[:, c2 * 512:(c2 + 1) * 512], ALU.add)
                        mk.dma(U(X1)[j * 128:(j + 1) * 128, :], xn, q="pool")

                    s5_prep(0)
                    for j in range(NOT):
                        if j + 1 < NOT:
                            s5_prep(j + 1)
                        s5_main(j)
                    mk.barrier()
            es_wo.close()
            if upto >= 6:
                ffn(0, X1, X2, NOT, mod0, False)
        if upto >= 7:
            es_l1 = ExitStack()
            with es_l1:
                mod1 = compute_mod(es_l1, 1)
                conformer(mod1)
                if upto >= 8:
                    ffn(1, X3, None, 16, mod1, True)
        mk.finish()
    return nc


_NC_CACHE = {}


def _prep_inputs(inputs):
    f = lambda a: np.ascontiguousarray(np.asarray(a, dtype=np.float32))
    x = f(inputs["x"])
    c = f(inputs["c"])
    shared = {
        "ada_w": f(inputs["ada_w"]),
        "ada_b": f(inputs["ada_b"]),
        "norm_mix_g": f(inputs["norm_mix_g"]),
        "norm_ffn_g": f(inputs["norm_ffn_g"]),
        "hy_w_in": f(inputs["hy_w_in"][0]),
        "convw": f(np.asarray(inputs["hy_conv_w"][0]).reshape(4, 24, 128).transpose(2, 1, 0).reshape(128, 96)),
        "convb": f(np.asarray(inputs["hy_conv_b"][0]).reshape(24, 128).T),
        "hy_dt_bias": f(inputs["hy_dt_bias"]),
        "hy_a_log": f(inputs["hy_a_log"]),
        "hy_d_skip": f(inputs["hy_d_skip"]),
        "hy_ssm_norm_g": f(inputs["hy_ssm_norm_g"]),
        "hy_w_out": f(inputs["hy_w_out"][0]),
        "rel_table": f(inputs["rel_table"]),
        "cv_w_pw1": f(inputs["cv_w_pw1"][0]),
        "b_pw1": f(np.asarray(inputs["cv_b_pw1"][0]).reshape(16, 128).T),
        "w_dw": f(np.asarray(inputs["cv_w_dw"][0]).reshape(31, 8, 128).transpose(2, 1, 0).reshape(128, 248)),
        "b_dw": f(np.asarray(inputs["cv_b_dw"][0]).reshape(8, 128).T),
        "ln_g": f(np.asarray(inputs["cv_ln_g"][0]).reshape(8, 128).T),
        "ln_b": f(np.asarray(inputs["cv_ln_b"][0]).reshape(8, 128).T),
        "cv_w_pw2": f(inputs["cv_w_pw2"][0]),
        "cv_b_pw2": f(inputs["cv_b_pw2"]),
        "ffn_w_gate": f(inputs["ffn_w_gate"]),
        "ffn_w_up": f(inputs["ffn_w_up"]),
        "ffn_w_down": f(inputs["ffn_w_down"]),
        "final_norm_g": f(np.asarray(inputs["final_norm_g"]).reshape(1, D)),
    }
    shared.update(host_consts())
    in_maps = []
    for core in range(8):
        b, hf = core // 2, core % 2
        xe = np.zeros((TE, D), np.float32)
        if hf == 1:
            xe[:] = x[b]
        else:
            xe[2048:] = x[b, :2048]
        m = dict(shared)
        m["xe"] = xe
        m["cb"] = f(c[b].reshape(8, 128).T)
        m["flag"] = np.full((128, 1), float(hf), np.float32)
        in_maps.append(m)
    return in_maps


def kernel(**inputs):
    if "nc" not in _NC_CACHE:
        _NC_CACHE["nc"] = build()
    nc = _NC_CACHE["nc"]
    in_maps = _prep_inputs(inputs)
    res = run_bass_kernel_spmd(nc, in_maps, core_ids=list(range(8)))
    out = np.zeros((4, 4096, D), np.float32)
    for core in range(8):
        b, hf = core // 2, core % 2
        out[b, hf * 2048:(hf + 1) * 2048] = res.results[core]["yout"]
    return out
```

```python
import math
import numpy as np
import concourse.bass as bass
import concourse.mybir as mybir
from concourse.bass_utils import run_bass_kernel_spmd
from contextlib import ExitStack

F32 = mybir.dt.float32
BF16 = mybir.dt.bfloat16
AF = mybir.ActivationFunctionType
ALU = mybir.AluOpType
AX = mybir.AxisListType

D = 1024
EPS = 1e-6
TE = 4096
OT0 = 15
NOT = 17
TO = NOT * 128
U0 = OT0 * 128
FFH = 2816
DILS = (1, 4, 16)
HY_SPL = (0, 2048, 5120, 5152, 8224, 9248, 10272)


class Buf:
    __slots__ = ("w", "r", "name", "wl")

    def __init__(self, name=""):
        self.w = None
        self.r = []
        self.wl = []
        self.name = name


class T:
    __slots__ = ("ap", "buf")

    def __init__(self, ap, buf):
        self.ap = ap
        self.buf = buf

    def __getitem__(self, key):
        return T(self.ap[key], self.buf)

    def re(self, s, **kw):
        return T(self.ap.rearrange(s, **kw), self.buf)

    def sub(self, ap_or_key, name=""):
        return T(self.ap[ap_or_key], Buf(name))


NSLOT = 8


class MK:
    def __init__(self, nc, es):
        self.nc = nc
        self.es = es
        self.eng = {"pe": nc.tensor, "act": nc.scalar, "dve": nc.vector, "pool": nc.gpsimd, "sp": nc.sync}
        self.prog = {k: [] for k in self.eng}
        self.sem = {}
        self.cnt = {}
        for k in self.eng:
            self.sem[k] = es.enter_context(nc.semaphore("s_" + k))
            self.cnt[k] = 0
        self.dsem = {}
        self.dcnt = {}
        self.dnext = {}
        for q in ("sp", "act", "pool"):
            self.dsem[q] = [es.enter_context(nc.semaphore("d_%s%d" % (q, i))) for i in range(NSLOT)]
            self.dcnt[q] = [0] * NSLOT
            self.dnext[q] = 0
        self.waited = {k: {} for k in self.eng}
        self.ninst = 0

    def _nm(self, name):
        self.uid = getattr(self, "uid", 0) + 1
        return "%s_u%d" % (name, self.uid)

    def sb(self, es, name, shape, dt):
        name = self._nm(name)
        t = es.enter_context(self.nc.sbuf_tensor(name, list(shape), dt))
        return T(t[:], Buf(name))

    def ps(self, es, name, shape, dt):
        name = self._nm(name)
        t = es.enter_context(self.nc.psum_tensor(name, list(shape), dt))
        return T(t[:], Buf(name))

    def dram(self, name, shape, dt, kind="Internal"):
        t = self.nc.dram_tensor(name, list(shape), dt, kind=kind)
        return T(t.ap(), Buf(name))

    def _semh(self, s):
        return self.sem[s[1]] if s[0] == "e" else self.dsem[s[1]][s[2]]

    def _emit(self, e, reads, writes, fn, inc=True, dma_q=None, par=False):
        deps = {}

        def add(d):
            if d is None:
                return
            s, v = d
            if deps.get(s, 0) < v:
                deps[s] = v

        mykey = ("e", e)
        for t in reads:
            add(t.buf.w)
            for w in t.buf.wl:
                add(w)
        for t in writes:
            w = t.buf.w
            if not par:
                if w is not None and not (w[0] == mykey and dma_q is None):
                    add(w)
                for w2 in t.buf.wl:
                    add(w2)
            elif w is not None and w[0][0] != "d":
                add(w)
            for r in t.buf.r:
                if not (r[0] == mykey and dma_q is None):
                    add(r)
        if dma_q is not None:
            q = dma_q
            slot = self.dnext[q]
            self.dnext[q] = (slot + 1) % NSLOT
            prev = self.dcnt[q][slot]
            if prev > 0:
                add((("d", q, slot), prev))
            self.dcnt[q][slot] = prev + 16
            myval = (("d", q, slot), prev + 16)
            semh = self.dsem[q][slot]
            incv = 16
            doinc = True
        else:
            if inc:
                self.cnt[e] += 1
                myval = (mykey, self.cnt[e])
            else:
                myval = (mykey, self.cnt[e] + 1)
            semh = self.sem[e]
            incv = 1
            doinc = inc
        waits = []
        wd = self.waited[e]
        for s, v in deps.items():
            if wd.get(s, 0) >= v:
                continue
            wd[s] = v
            waits.append((self._semh(s), v))

        def run(eh, waits=waits, fn=fn, semh=semh, incv=incv, doinc=doinc):
            for sh, v in waits:
                eh.wait_ge(sh, v)
            ins = fn(eh)
            if doinc:
                ins.then_inc(semh, incv)

        self.prog[e].append(run)
        self.ninst += 1
        wbufs = set(id(t.buf) for t in writes)
        for t in writes:
            if par:
                t.buf.wl.append(myval)
            else:
                t.buf.w = myval
                t.buf.wl = []
                t.buf.r = []
        for t in reads:
            if id(t.buf) not in wbufs:
                t.buf.r.append(myval)
                if len(t.buf.r) > 48:
                    m = {}
                    for s, v in t.buf.r:
                        if m.get(s, 0) < v:
                            m[s] = v
                    t.buf.r = list(m.items())
        return myval

    @staticmethod
    def _a(x):
        return x.ap if isinstance(x, T) else x

    def mm(self, out, lhsT, rhs, start=True, stop=True):
        self._emit("pe", [lhsT, rhs], [out],
                   lambda eh: eh.matmul(out.ap, lhsT.ap, rhs.ap, start=start, stop=stop), inc=stop)

    def transpose(self, out, in_, ident):
        self._emit("pe", [in_, ident], [out], lambda eh: eh.transpose(out.ap, in_.ap, ident.ap))

    def act(self, out, in_, func, bias=None, scale=1.0, accum=None):
        reads = [in_] + [x for x in (bias, scale) if isinstance(x, T)]
        writes = [out] + ([accum] if accum is not None else [])
        kw = {}
        if bias is not None:
            kw["bias"] = self._a(bias)
        if accum is not None:
            kw["accum_out"] = accum.ap
        sc = self._a(scale)
        self._emit("act", reads, writes, lambda eh: eh.activation(out.ap, in_.ap, func, scale=sc, **kw))

    def tt(self, e, out, in0, in1, op):
        self._emit(e, [in0, in1], [out], lambda eh: eh.tensor_tensor(out.ap, in0.ap, in1.ap, op))

    def ts(self, e, out, in0, s1, s2, op0, op1=None):
        reads = [in0] + [x for x in (s1, s2) if isinstance(x, T)]
        a1, a2 = self._a(s1), self._a(s2)
        if op1 is None:
            self._emit(e, reads, [out], lambda eh: eh.tensor_scalar(out.ap, in0.ap, a1, None, op0))
        else:
            self._emit(e, reads, [out], lambda eh: eh.tensor_scalar(out.ap, in0.ap, a1, a2, op0, op1))

    def stt(self, e, out, in0, scalar, in1, op0, op1):
        reads = [in0, in1] + ([scalar] if isinstance(scalar, T) else [])
        sc = self._a(scalar)
        self._emit(e, reads, [out], lambda eh: eh.scalar_tensor_tensor(out.ap, in0.ap, sc, in1.ap, op0, op1))

    def copy(self, e, out, in_):
        if e == "act":
            self._emit(e, [in_], [out], lambda eh: eh.copy(out.ap, in_.ap))
        else:
            self._emit(e, [in_], [out], lambda eh: eh.tensor_copy(out.ap, in_.ap))

    def memset(self, e, out, val):
        self._emit(e, [], [out], lambda eh: eh.memset(out.ap, val))

    def recip(self, out, in_):
        self._emit("dve", [in_], [out], lambda eh: eh.reciprocal(out.ap, in_.ap))

    def dma(self, out, in_, q="sp", par=False):
        return self._emit(q, [in_], [out], lambda eh: eh.dma_start(out=out.ap, in_=in_.ap), dma_q=q, par=par)

    def barrier(self):
        targets = [(("e", k), self.cnt[k]) for k in self.eng if self.cnt[k] > 0]
        for q in self.dsem:
            for s in range(NSLOT):
                if self.dcnt[q][s] > 0:
                    targets.append((("d", q, s), self.dcnt[q][s]))
        for e in self.eng:
            waits = []
            wd = self.waited[e]
            for s, v in targets:
                if s == ("e", e) or wd.get(s, 0) >= v:
                    continue
                wd[s] = v
                waits.append((self._semh(s), v))

            def run(eh, waits=waits):
                for sh, v in waits:
                    eh.wait_ge(sh, v)

            self.prog[e].append(run)

    def finish(self):
        self.barrier()
        prog = self.prog
        with self.nc.Block() as block:
            @block.sync
            def _(eh):
                for f in prog["sp"]:
                    f(eh)

            @block.tensor
            def _(eh):
                for f in prog["pe"]:
                    f(eh)

            @block.scalar
            def _(eh):
                for f in prog["act"]:
                    f(eh)

            @block.vector
            def _(eh):
                for f in prog["dve"]:
                    f(eh)

            @block.gpsimd
            def _(eh):
                for f in prog["pool"]:
                    f(eh)


def bc(t, shape_steps, off=0):
    return T(bass.AP(t.ap.tensor, off, [list(x) for x in shape_steps]), t.buf)


def _t5_bucket_np(dist):
    dist = np.asarray(dist, np.int64)
    n = np.maximum(dist, 1).astype(np.float64)
    large = 16 + np.log(n / 16.0) / math.log(2048 / 16) * 16
    large = np.minimum(np.floor(large + 1e-9).astype(np.int64), 31)
    return np.where(dist < 16, dist, large)


def host_consts():
    c = {}
    c["ident"] = np.eye(128, dtype=np.float32)
    c["jrev"] = np.ascontiguousarray(np.eye(128, dtype=np.float32)[::-1])
    k = np.arange(128)
    c["tri"] = (k[:, None] <= k[None, :]).astype(np.float32)
    c["ustr"] = (k[:, None] > k[None, :]).astype(np.float32)
    c["ones"] = np.ones((128, 128), np.float32)
    oh = np.zeros((3, 32, 384), np.float32)
    pm = np.zeros((3, 16, 384), np.float32)
    for g, d in enumerate(DILS):
        for u in range(129):
            b = int(_t5_bucket_np(u * d))
            oh[g, b, 128 + u] = 1.0
            pm[g, :, 128 + u] = 1.0
    c["onehot"] = oh
    c["pmask"] = pm
    return c


def build(upto=99, taps=()):
    nc = bass.Bass("TRN2", target_bir_lowering=False)
    es0 = ExitStack()
    with es0:
        mk = MK(nc, es0)
        taps = set(taps)

        def din(name, shape, dt=F32):
            return mk.dram(name, shape, dt, kind="ExternalInput")

        def dscr(name, shape, dt):
            return mk.dram(name, shape, dt, kind=("ExternalOutput" if name in taps else "Internal"))

        xe = din("xe", [TE, D])
        cb = din("cb", [128, 8])
        flag_d = din("flag", [128, 1])
        ada_w = din("ada_w", [2, D, 6 * D])
        ada_b = din("ada_b", [2, 6 * D])
        nmg = din("norm_mix_g", [2, D])
        nfg = din("norm_ffn_g", [2, D])
        w_in = din("hy_w_in", [D, 10272])
        convw = din("convw", [128, 24 * 4])
        convb = din("convb", [128, 24])
        dtb = din("hy_dt_bias", [1, 32])
        alog = din("hy_a_log", [1, 32])
        dskip = din("hy_d_skip", [1, 32])
        ssmg = din("hy_ssm_norm_g", [1, 2048])
        w_out = din("hy_w_out", [3072, D])
        relt = din("rel_table", [32, 48])
        w_pw1 = din("cv_w_pw1", [D, 2 * D])
        b_pw1 = din("b_pw1", [128, 16])
        w_dw = din("w_dw", [128, 8 * 31])
        b_dw = din("b_dw", [128, 8])
        ln_g = din("ln_g", [128, 8])
        ln_b = din("ln_b", [128, 8])
        w_pw2 = din("cv_w_pw2", [D, D])
        b_pw2 = din("cv_b_pw2", [1, D])
        w_gate = din("ffn_w_gate", [2, D, FFH])
        w_up = din("ffn_w_up", [2, D, FFH])
        w_down = din("ffn_w_down", [2, FFH, D])
        fng = din("final_norm_g", [1, D])
        ident_d = din("ident", [128, 128])
        jrev_d = din("jrev", [128, 128])
        tri_d = din("tri", [128, 128])
        ustr_d = din("ustr", [128, 128])
        ones_d = din("ones", [128, 128])
        onehot_d = din("onehot", [3, 32, 384])
        pmask_d = din("pmask", [3, 16, 384])
        yout = mk.dram("yout", [2048, D], F32, kind="ExternalOutput")

        XSB = dscr("XSB", [TE, 2560], BF16)
        BCF = dscr("BCF", [1024, TE], BF16)
        ZS = dscr("ZS", [TO, 2048], F32)
        Qd = dscr("Qd", [3, 1024, TO], BF16)
        Kd = dscr("Kd", [3, 1024, TE], BF16)
        VA = dscr("VA", [TE, 1056], BF16)
        YNT = dscr("YNT", [2048, TO], BF16)
        OA = dscr("OA", [3, TO, 1040], F32)
        ED = dscr("ED", [3, 128, 16 * 256], F32)
        PM = dscr("PM", [48, 384], F32)
        X1 = dscr("X1", [TO, D], F32)
        X2 = dscr("X2", [TO, D], F32)
        X3 = dscr("X3", [2048, D], F32)
        DBG = dscr("DBG", [128, 4096], F32)

        identf = mk.sb(es0, "identf", [128, 128], F32)
        identb = mk.sb(es0, "identb", [128, 128], BF16)
        jrev = mk.sb(es0, "jrev_s", [128, 128], F32)
        tri = mk.sb(es0, "tri_s", [128, 128], F32)
        ustr = mk.sb(es0, "ustr_s", [128, 128], F32)
        ones = mk.sb(es0, "ones_s", [128, 128], F32)
        flag = mk.sb(es0, "flag_s", [128, 1], F32)
        mk.dma(identf, ident_d)
        mk.dma(identb, ident_d, q="pool")
        mk.dma(jrev, jrev_d)
        mk.dma(tri, tri_d)
        mk.dma(ustr, ustr_d)
        mk.dma(ones, ones_d)
        mk.dma(flag, flag_d)
        cs = mk.sb(es0, "cs", [128, 8], F32)
        csb = mk.sb(es0, "csb", [128, 8, 128], F32)
        mk.dma(cs, cb)
        mk.act(cs, cs, AF.Silu)
        mk.copy("dve", csb, bc(cs, [[cs.ap.ap[0][0], 128], [1, 8], [0, 128]], cs.ap.offset))

        def wview(Wd, kc):
            return Wd.re("(k p) n -> p k n", p=128)

        def wload(dst, Wd, c0, n):
            mk.dma(dst, wview(Wd, 0)[:, :, c0:c0 + n], q="pool")

        def compute_mod(es, l):
            modt = mk.sb(es, "mod%d" % l, [128, 6, D], F32)
            with ExitStack() as esl:
                wa = [mk.sb(esl, "adaw%d" % i, [128, 8, 512], F32) for i in range(4)]
                pp = [mk.ps(esl, "modps%d" % i, [128, 512], F32) for i in range(4)]
                bt = mk.sb(esl, "adab", [128, 6 * D], F32)
                gt = mk.sb(esl, "ngt", [128, 2, D], F32)
                mk.dma(bt, bc(ada_b, [[0, 128], [1, 6 * D]], l * 6 * D))
                mk.dma(gt[:, 0, :], bc(nmg, [[0, 128], [1, D]], l * D))
                mk.dma(gt[:, 1, :], bc(nfg, [[0, 128], [1, D]], l * D))
                awl = T(ada_w.ap[l], ada_w.buf).re("(k p) n -> p k n", p=128)
                for nb in range(12):
                    w = wa[nb % 4]
                    mk.dma(w, awl[:, :, nb * 512:(nb + 1) * 512], q=("sp" if nb % 2 == 0 else "act"))
                    p = pp[nb % 4]
                    for k in range(8):
                        mk.mm(p, csb[:, k, :], w[:, k, :], start=(k == 0), stop=(k == 7))
                    mk.tt("dve", modt[:, nb // 2, (nb % 2) * 512:(nb % 2 + 1) * 512], p,
                          bt[:, nb * 512:(nb + 1) * 512], ALU.add)
                for (j, gi) in ((1, 0), (4, 1)):
                    mk.stt("dve", modt[:, j, :], modt[:, j, :], 1.0, gt[:, gi, :], ALU.add, ALU.mult)
                mk.barrier()
            return {"S1": modt[:, 0, :], "G1": modt[:, 1, :], "GA1": modt[:, 2, :],
                    "S2": modt[:, 3, :], "G2": modt[:, 4, :], "GA2": modt[:, 5, :]}

        class NormCtx:
            def __init__(self, es, tag, nbuf=2, npt=2, nt32=None):
                self.nbuf = nbuf
                self.npt = npt
                self.nt32 = nbuf if nt32 is None else nt32
                self.junk = mk.sb(es, tag + "junk", [128, D], F32)
                self.t32 = [mk.sb(es, tag + "t32_%d" % i, [128, D], F32) for i in range(self.nt32)]
                self.hb = [mk.sb(es, tag + "hb%d" % i, [128, D], BF16) for i in range(nbuf)]
                self.ss = [mk.sb(es, tag + "ss%d" % i, [128, 4], F32) for i in range(nbuf)]
                self.pt = [mk.ps(es, tag + "pt%d" % i, [128, 1024], BF16) for i in range(npt)]
                self.i = 0

        def rstd_of(xt, ss, junk, n):
            mk.memset("dve", ss[:, 0:1], 0.0)
            mk.act(junk, xt, AF.Square, accum=ss[:, 0:1])
            mk.ts("dve", ss[:, 1:2], ss[:, 0:1], 1.0 / n, EPS, ALU.mult, ALU.add)
            mk.act(ss[:, 2:3], ss[:, 1:2], AF.Ln)
            mk.act(ss[:, 3:4], ss[:, 2:3], AF.Exp, scale=-0.5)
            return ss[:, 3:4]

        def norm_tile(ctx, xt, G, S, dst):
            i = ctx.i
            ctx.i += 1
            ss, t32, hb, pt = ctx.ss[i % ctx.nbuf], ctx.t32[i % ctx.nt32], ctx.hb[i % ctx.nbuf], ctx.pt[i % ctx.npt]
            r = rstd_of(xt, ss, ctx.junk, D)
            mk.stt("dve", t32, xt, r, G, ALU.mult, ALU.mult)
            mk.tt("pool", hb, t32, S, ALU.add)
            for k in range(8):
                mk.transpose(pt[:, k * 128:(k + 1) * 128], hb[:, k * 128:(k + 1) * 128], identb)
            mk.copy("act", dst, pt.re("p (k t) -> p k t", k=8))

        def norm_A(ctx, xt, G, S):
            i = ctx.i
            ctx.i += 1
            ss, t32, hb = ctx.ss[i % ctx.nbuf], ctx.t32[i % ctx.nt32], ctx.hb[i % ctx.nbuf]
            r = rstd_of(xt, ss, ctx.junk, D)
            mk.stt("dve", t32, xt, r, G, ALU.mult, ALU.mult)
            mk.tt("pool", hb, t32, S, ALU.add)
            return i

        def norm_B(ctx, i, dst):
            hb, pt = ctx.hb[i % ctx.nbuf], ctx.pt[i % ctx.npt]
            for k in range(8):
                mk.transpose(pt[:, k * 128:(k + 1) * 128], hb[:, k * 128:(k + 1) * 128], identb)
            mk.copy("act", dst, pt.re("p (k t) -> p k t", k=8))

        def BL(t, axis, shape):
            return T(t.ap.unsqueeze(axis).to_broadcast(list(shape)), t.buf)

        def ROW(dt_, n, off=0):
            return T(bass.AP(dt_.ap.tensor, off, [[0, 128], [1, n]]), Buf())

        def U(t):
            return T(t.ap, Buf())

        def UA(t, off, steps):
            return T(bass.AP(t.ap.tensor, off, [list(x) for x in steps]), Buf())

        class PRot:
            def __init__(self, es, n, tag):
                self.p = [mk.ps(es, "%s%d" % (tag, i), [128, 512], F32) for i in range(n)]
                self.i = 0

            def __call__(self):
                p = self.p[self.i % len(self.p)]
                self.i += 1
                return p

        ecnt = [0]

        def evac_eng():
            ecnt[0] += 1
            return "act" if ecnt[0] % 2 else "dve"

        def dbg_out(src_sb, ncols):
            mk.dma(U(DBG)[:, 0:ncols], src_sb)

        HTF = dscr("HTF", [D, TO], BF16)
        XMID = dscr("XMID", [TO, D], F32)
        UC = dscr("UC", [D, 2048], F32)

        def ffn(l, Xin, Xout, ntiles, mod, final):
            TG = 4
            wg_l = T(w_gate.ap[l], w_gate.buf).re("(k p) n -> p k n", p=128)
            wu_l = T(w_up.ap[l], w_up.buf).re("(k p) n -> p k n", p=128)
            wd_l = T(w_down.ap[l], w_down.buf).re("(k p) n -> p k n", p=128)
            HTFv = U(HTF).re("(k p) t -> p k t", p=128)
            for ps_ in range(2):
                f0 = ps_ * 11
                with ExitStack() as es:
                    Wg = mk.sb(es, "Wg", [128, 8, 1408], BF16)
                    Wu = mk.sb(es, "Wu", [128, 8, 1408], BF16)
                    Wd = mk.sb(es, "Wd", [128, 11, D], BF16)
                    for i in range(2):
                        mk.dma(Wg[:, :, i * 704:(i + 1) * 704], wg_l[:, :, f0 * 128 + i * 704:f0 * 128 + (i + 1) * 704], q="pool")
                        mk.dma(Wu[:, :, i * 704:(i + 1) * 704], wu_l[:, :, f0 * 128 + i * 704:f0 * 128 + (i + 1) * 704], q="pool")
                    mk.dma(Wd[:, 0:6, :], wd_l[:, f0:f0 + 6, :], q="pool")
                    mk.dma(Wd[:, 6:11, :], wd_l[:, f0 + 6:f0 + 11, :], q="pool")
                    nctx = NormCtx(es, "nf", 4, 2, 2) if ps_ == 0 else None
                    nids = {}
                    pf = PRot(es, 6, "pff")
                    xg = [mk.sb(es, "xg%d" % i, [128, TG, D], F32) for i in range(2)]
                    hTf = [mk.sb(es, "hTf%d" % i, [128, 8, TG * 128], BF16) for i in range(2)]
                    hid = [mk.sb(es, "hid%d" % i, [128, 11, TG * 128], BF16) for i in range(2)]
                    sgl = [mk.sb(es, "sgl%d" % i, [128, TG * 128], F32) for i in range(2)]
                    tmpf = [mk.sb(es, "tmpf%d" % i, [128, 512], F32) for i in range(2)]
                    xnf = [mk.sb(es, "xnf%d" % i, [128, D], F32) for i in range(2)]
                    if final and ps_ == 1:
                        fng_b = mk.sb(es, "fng_b", [128, D], F32)
                        mk.dma(fng_b, ROW(fng, D))
                        ssf = mk.sb(es, "ssf", [128, 4], F32)
                        junkf = mk.sb(es, "junkf", [128, D], F32)
                        yo = [mk.sb(es, "yo%d" % i, [128, D], F32) for i in range(2)]
                    src = Xin if ps_ == 0 else XMID
                    groups = [(g0, min(TG, ntiles - g0)) for g0 in range(0, ntiles, TG)]

                    def f_NA(gi):
                        g0, nt = groups[gi]
                        xg_ = xg[gi % 2]
                        for i in range(nt):
                            mk.dma(xg_[:, i, :], U(src)[(g0 + i) * 128:(g0 + i + 1) * 128, :], par=(i > 0))

                    def f_NA2(gi):
                        g0, nt = groups[gi]
                        xg_ = xg[gi % 2]
                        if ps_ == 0:
                            nids[gi] = [norm_A(nctx, xg_[:, i, :], mod["G2"], mod["S2"]) for i in range(nt)]

                    def f_NB(gi):
                        g0, nt = groups[gi]
                        N = nt * 128
                        hT_ = hTf[gi % 2]
                        if ps_ == 0:
                            for i in range(nt):
                                norm_B(nctx, nids[gi][i], hT_[:, :, i * 128:(i + 1) * 128])
                            mk.dma(HTFv[:, :, g0 * 128:g0 * 128 + N], hT_[:, :, 0:N])
                        else:
                            mk.dma(hT_[:, :, 0:N], HTFv[:, :, g0 * 128:g0 * 128 + N])

                    def f_GU(gi):
                        g0, nt = groups[gi]
                        N = nt * 128
                        hT_ = hTf[gi % 2]
                        hid_ = hid[gi % 2]
                        for ft in range(11):
                            if ft == 5 and gi + 1 < len(groups):
                                f_NA2(gi + 1)
                            pg = pf()
                            pu = pf()
                            for k in range(8):
                                mk.mm(pg[:, 0:N], Wg[:, k, ft * 128:(ft + 1) * 128], hT_[:, k, 0:N], start=(k == 0), stop=(k == 7))
                            for k in range(8):
                                mk.mm(pu[:, 0:N], Wu[:, k, ft * 128:(ft + 1) * 128], hT_[:, k, 0:N], start=(k == 0), stop=(k == 7))
                            sg_ = sgl[ft % 2]
                            mk.act(sg_[:, 0:N], pg[:, 0:N], AF.Silu)
                            mk.tt("dve", hid_[:, ft, 0:N], sg_[:, 0:N], pu[:, 0:N], ALU.mult)

                    def f_DN(gi):
                        g0, nt = groups[gi]
                        xg_ = xg[gi % 2]
                        hid_ = hid[gi % 2]
                        for i in range(nt):
                            xn = xnf[i % 2]
                            for c2 in range(2):
                                p = pf()
                                for ft in range(11):
                                    mk.mm(p, hid_[:, ft, i * 128:(i + 1) * 128], Wd[:, ft, c2 * 512:(c2 + 1) * 512],
                                          start=(ft == 0), stop=(ft == 10))
                                t_ = tmpf[c2]
                                mk.tt("dve", t_, p, mod["GA2"][:, c2 * 512:(c2 + 1) * 512], ALU.mult)
                                mk.tt("pool", xn[:, c2 * 512:(c2 + 1) * 512], t_, xg_[:, i, c2 * 512:(c2 + 1) * 512], ALU.add)
                            row = slice((g0 + i) * 128, (g0 + i + 1) * 128)
                            if ps_ == 0:
                                mk.dma(U(XMID)[row, :], xn)
                            elif not final:
                                mk.dma(U(Xout)[row, :], xn)
                            else:
                                r = rstd_of(xn, ssf, junkf, D)
                                y_ = yo[i % 2]
                                mk.stt("dve", y_, xn, r, fng_b, ALU.mult, ALU.mult)
                                mk.dma(yout[row, :], y_)

                    f_NA(0)
                    f_NA2(0)
                    f_NB(0)
                    for gi in range(len(groups)):
                        if gi + 1 < len(groups):
                            f_NA(gi + 1)
                        f_GU(gi)
                        if gi + 1 < len(groups):
                            f_NB(gi + 1)
                        f_DN(gi)
                    mk.barrier()

        def conformer(mod):
            with ExitStack() as es:
                hT1 = mk.sb(es, "hT1", [128, 8, TO], BF16)
                with ExitStack() as es2:
                    nctx = NormCtx(es2, "n1", 4, 3)
                    xt = [mk.sb(es2, "xt1_%d" % i, [128, D], F32) for i in range(4)]
                    def ncA(j):
                        x_ = xt[j % 4]
                        mk.dma(x_, U(X2)[j * 128:(j + 1) * 128, :])
                        return norm_A(nctx, x_, mod["G1"], mod["S1"])

                    ids = {0: ncA(0)}
                    for j in range(NOT):
                        if j + 1 < NOT:
                            ids[j + 1] = ncA(j + 1)
                        norm_B(nctx, ids[j], hT1[:, :, j * 128:(j + 1) * 128])
                    mk.barrier()
                pf = PRot(es, 6, "pfc")
                W1 = mk.sb(es, "W1", [128, 8, 2048], BF16)
                for i in range(4):
                    wload(W1[:, :, i * 512:(i + 1) * 512], w_pw1, i * 512, 512)
                b1 = mk.sb(es, "b1", [128, 16], F32)
                wdw = mk.sb(es, "wdw", [128, 8 * 31], F32)
                bdw = mk.sb(es, "bdw", [128, 8], F32)
                mk.dma(b1, b_pw1)
                mk.dma(wdw, w_dw)
                mk.dma(bdw, b_dw)
                ubf = [mk.sb(es, "ubf%d" % i, [128, 30 + 2048], BF16) for i in range(2)]
                dgl = [mk.sb(es, "dg%d" % i, [128, 31, 128], BF16) for i in range(2)]
                ucs = [mk.sb(es, "ucs%d" % i, [128, 2048], F32) for i in range(2)]
                sgm = [mk.sb(es, "sgm%d" % i, [128, 512], F32) for i in range(2)]
                t30 = mk.sb(es, "t30", [128, 32], F32)
                pieces = [(96, 32)] + [(128 + i * 512, 512) for i in range(4)]

                def cA(ct):
                    u_ = ubf[ct % 2]
                    for pi, (u0, n) in enumerate(pieces):
                        pa = pf()
                        pg = pf()
                        for k in range(8):
                            mk.mm(pa[:, 0:n], W1[:, k, ct * 128:(ct + 1) * 128], hT1[:, k, u0:u0 + n], start=(k == 0), stop=(k == 7))
                        for k in range(8):
                            mk.mm(pg[:, 0:n], W1[:, k, 1024 + ct * 128:1024 + (ct + 1) * 128], hT1[:, k, u0:u0 + n],
                                  start=(k == 0), stop=(k == 7))
                        sg_ = sgm[pi % 2]
                        mk.act(sg_[:, 0:n], pg[:, 0:n], AF.Sigmoid, bias=b1[:, 8 + ct:9 + ct])
                        if pi == 0:
                            mk.stt("dve", t30, pa[:, 0:32], b1[:, ct:ct + 1], sg_[:, 0:32], ALU.add, ALU.mult)
                            mk.act(u_[:, 0:30], t30[:, 2:32], AF.Identity, scale=flag)
                        else:
                            o = 30 + u0 - 128
                            mk.stt("dve", u_[:, o:o + n], pa[:, 0:n], b1[:, ct:ct + 1], sg_[:, 0:n], ALU.add, ALU.mult)
                    dg = dgl[ct % 2]
                    for k in range(31):
                        mk.act(dg[:, k, :], identb, AF.Identity, scale=wdw[:, ct * 31 + k:ct * 31 + k + 1])

                def cB(ct):
                    u_ = ubf[ct % 2]
                    dg = dgl[ct % 2]
                    uc_ = ucs[ct % 2]
                    for t4 in range(4):
                        p = pf()
                        for k in range(31):
                            mk.mm(p, dg[:, k, :], u_[:, k + t4 * 512:k + t4 * 512 + 512], start=(k == 0), stop=(k == 30))
                        mk.act(uc_[:, t4 * 512:(t4 + 1) * 512], p, AF.Identity, bias=bdw[:, ct:ct + 1])
                    mk.dma(U(UC)[ct * 128:(ct + 1) * 128, :], uc_)

                cA(0)
                for ct in range(8):
                    if ct + 1 < 8:
                        cA(ct + 1)
                    cB(ct)
                mk.barrier()
            with ExitStack() as es:
                W2 = mk.sb(es, "W2", [128, 8, D], BF16)
                wload(W2[:, :, 0:512], w_pw2, 0, 512)
                wload(W2[:, :, 512:1024], w_pw2, 512, 512)
                b2_b = mk.sb(es, "b2_b", [128, D], F32)
                mk.dma(b2_b, ROW(b_pw2, D))
                lng = mk.sb(es, "lng", [128, 8], F32)
                lnb = mk.sb(es, "lnb", [128, 8], F32)
                mk.dma(lng, ln_g)
                mk.dma(lnb, ln_b)
                ucg = [mk.sb(es, "ucg%d" % i, [128, 8, 512], F32) for i in range(2)]
                sq = mk.sb(es, "sq", [128, 8, 512], F32)
                mean_l = [mk.sb(es, "mean%d" % i, [128, 512], F32) for i in range(2)]
                ex2_l = [mk.sb(es, "ex2%d" % i, [128, 512], F32) for i in range(2)]
                rs_l = [mk.sb(es, "rs%d" % i, [128, 512], F32) for i in range(2)]
                t1 = [mk.sb(es, "t1_%d" % i, [128, 512], F32) for i in range(2)]
                vfm_l = [mk.sb(es, "vfm%d" % i, [128, 8, 512], BF16) for i in range(2)]
                x2t = [mk.sb(es, "x2t%d" % i, [128, D], F32) for i in range(2)]
                xn1 = [mk.sb(es, "xn1_%d" % i, [128, D], F32) for i in range(2)]
                tmpc = [mk.sb(es, "tmpc%d" % i, [128, 512], F32) for i in range(2)]
                p_m = mk.ps(es, "p_m", [128, 512], F32)
                p_q = mk.ps(es, "p_q", [128, 512], F32)
                pf = PRot(es, 4, "pfc2")
                UCv = U(UC).re("(c p) t -> p c t", p=128)

                def cL(tg):
                    uc_ = ucg[tg % 2]
                    mean, ex2, rs, vfm = mean_l[tg % 2], ex2_l[tg % 2], rs_l[tg % 2], vfm_l[tg % 2]
                    mk.dma(uc_, UCv[:, :, tg * 512:(tg + 1) * 512])
                    mk.act(sq, uc_, AF.Square)
                    for ct in range(8):
                        mk.mm(p_m, ones, uc_[:, ct, :], start=(ct == 0), stop=(ct == 7))
                    for ct in range(8):
                        mk.mm(p_q, ones, sq[:, ct, :], start=(ct == 0), stop=(ct == 7))
                    mk.act(mean, p_m, AF.Identity, scale=1.0 / D)
                    mk.act(ex2, p_q, AF.Identity, scale=1.0 / D)
                    mk.tt("dve", rs, mean, mean, ALU.mult)
                    mk.tt("dve", rs, ex2, rs, ALU.subtract)
                    mk.ts("dve", rs, rs, 1.0, EPS, ALU.mult, ALU.add)
                    mk.act(rs, rs, AF.Ln)
                    mk.act(rs, rs, AF.Exp, scale=-0.5)
                    for ct in range(8):
                        t_ = t1[ct % 2]
                        mk.tt("dve", t_, uc_[:, ct, :], mean, ALU.subtract)
                        mk.tt("pool", t_, t_, rs, ALU.mult)
                        mk.act(vfm[:, ct, :], t_, AF.Silu, scale=lng[:, ct:ct + 1], bias=lnb[:, ct:ct + 1])

                def cM(tg):
                    vfm = vfm_l[tg % 2]
                    for i in range(4):
                        ti = tg * 4 + i
                        x_ = x2t[i % 2]
                        mk.dma(x_, U(X2)[(1 + ti) * 128:(2 + ti) * 128, :])
                        xn = xn1[i % 2]
                        for c2 in range(2):
                            p = pf()
                            for ct in range(8):
                                mk.mm(p, vfm[:, ct, i * 128:(i + 1) * 128], W2[:, ct, c2 * 512:(c2 + 1) * 512],
                                      start=(ct == 0), stop=(ct == 7))
                            t_ = tmpc[c2]
                            mk.tt("dve", t_, p, b2_b[:, c2 * 512:(c2 + 1) * 512], ALU.add)
                            mk.tt("dve", t_, t_, mod["GA1"][:, c2 * 512:(c2 + 1) * 512], ALU.mult)
                            mk.tt("pool", xn[:, c2 * 512:(c2 + 1) * 512], t_, x_[:, c2 * 512:(c2 + 1) * 512], ALU.add)
                        mk.dma(U(X3)[ti * 128:(ti + 1) * 128, :], xn, q="pool")

                cL(0)
                for tg in range(4):
                    if tg + 1 < 4:
                        cL(tg + 1)
                    cM(tg)
                mk.barrier()

        es_l0 = ExitStack()
        with es_l0:
            mod0 = compute_mod(es_l0, 0)
            es_dt = ExitStack()
            es_dt.__enter__()
            dtall = mk.sb(es_dt, "dtall", [128, 32, 32], F32)
            dtAall = mk.sb(es_dt, "dtAall", [128, 32, 32], F32)
            if upto >= 1:
                es_s12 = ExitStack()
                with es_s12:
                    hT = mk.sb(es_s12, "hT", [128, 8, TE], BF16)
                    es_z = ExitStack()
                    es_z.__enter__()
                    Wz = mk.sb(es_z, "Wz", [128, 8, 2048], BF16)
                    for i in range(4):
                        wload(Wz[:, :, i * 512:(i + 1) * 512], w_in, i * 512, 512)
                    Wdt = mk.sb(es_z, "Wdt", [128, 8, 32], BF16)
                    wload(Wdt, w_in, 5120, 32)
                    with ExitStack() as es:
                        nctx = NormCtx(es, "n0", 4, 3)
                        xt = [mk.sb(es, "xt%d" % i, [128, D], F32) for i in range(4)]
                        def n1A(t):
                            x_ = xt[t % 4]
                            mk.dma(x_, xe[t * 128:(t + 1) * 128, :])
                            return norm_A(nctx, x_, mod0["G1"], mod0["S1"])

                        ids = {0: n1A(0)}
                        for t in range(32):
                            if t + 1 < 32:
                                ids[t + 1] = n1A(t + 1)
                            norm_B(nctx, ids[t], hT[:, :, t * 128:(t + 1) * 128])
                        mk.barrier()
                    if upto >= 2:
                        if True:
                            with ExitStack() as es:
                                pf = PRot(es, 6, "pf2b")
                                zst = [mk.sb(es, "zst%d" % i, [128, 2048], F32) for i in range(2)]
                                dtb_b = mk.sb(es, "dtb_b", [128, 32], F32)
                                A_b = mk.sb(es, "A_b", [128, 32], F32)
                                mk.dma(dtb_b, ROW(dtb, 32))
                                mk.dma(A_b, ROW(alog, 32))
                                mk.act(A_b, A_b, AF.Exp)
                                mk.ts("dve", A_b, A_b, -1.0, 0.0, ALU.mult, ALU.add)
                                for half in range(2):
                                    p = pf()
                                    for tl in range(16):
                                        t = half * 16 + tl
                                        for k in range(8):
                                            mk.mm(p[:, tl * 32:(tl + 1) * 32], hT[:, k, t * 128:(t + 1) * 128], Wdt[:, k, :],
                                                  start=(k == 0), stop=(k == 7))
                                    tmp = dtall[:, half * 16:(half + 1) * 16, :]
                                    mk.tt("dve", tmp, p.re("p (t h) -> p t h", t=16), BL(dtb_b, 1, [128, 16, 32]), ALU.add)
                                    mk.act(tmp, tmp, AF.Exp)
                                    mk.act(tmp, tmp, AF.Ln, bias=1.0)
                                    mk.tt("dve", dtAall[:, half * 16:(half + 1) * 16, :], tmp, BL(A_b, 1, [128, 16, 32]), ALU.mult)
                                for j in range(NOT):
                                    t = OT0 + j
                                    zs = zst[j % 2]
                                    for c4 in range(4):
                                        p = pf()
                                        for k in range(8):
                                            mk.mm(p, hT[:, k, t * 128:(t + 1) * 128], Wz[:, k, c4 * 512:(c4 + 1) * 512],
                                                  start=(k == 0), stop=(k == 7))
                                        mk.act(zs[:, c4 * 512:(c4 + 1) * 512], p, AF.Silu)
                                    mk.dma(U(ZS)[j * 128:(j + 1) * 128, :], zs)
                                mk.barrier()
                        es_z.close()
                        with ExitStack() as es:
                            pf = PRot(es, 5, "pf2a")
                            ptb = [mk.ps(es, "ptb%d" % i, [128, 1024], BF16) for i in range(2)]
                            cw = mk.sb(es, "cw", [128, 96], F32)
                            cbv = mk.sb(es, "cbv", [128, 24], F32)
                            mk.dma(cw, convw)
                            mk.dma(cbv, convb)
                            Wc = [mk.sb(es, "Wc%d" % i, [128, 8, 512], BF16) for i in range(2)]
                            raw = [mk.sb(es, "raw%d" % i, [128, 3 + TE], F32) for i in range(2)]
                            acc = [mk.sb(es, "acc%d" % i, [128, 2048], F32) for i in range(2)]
                            xc = [mk.sb(es, "xc%d" % i, [128, TE], BF16) for i in range(2)]
                            stg = [mk.sb(es, "stg%d" % i, [128, 32, 128], BF16) for i in range(2)]
                            for r_ in raw:
                                mk.memset("pool", r_[:, 0:3], 0.0)
                            XSBv = U(XSB).re("(t p) c -> p t c", p=128)
                            def s2a_A(ct):
                                cbk, ci = divmod(ct, 4)
                                W = Wc[cbk % 2]
                                if ci == 0:
                                    wload(W, w_in, 2048 + cbk * 512, 512)
                                rw = raw[ct % 2]
                                for tt in range(8):
                                    p = pf()
                                    for k in range(8):
                                        mk.mm(p, W[:, k, ci * 128:(ci + 1) * 128], hT[:, k, tt * 512:(tt + 1) * 512],
                                              start=(k == 0), stop=(k == 7))
                                    dst = rw[:, 3 + tt * 512:3 + (tt + 1) * 512]
                                    if tt < 4:
                                        mk.act(dst, p, AF.Identity, scale=flag)
                                    else:
                                        mk.copy("act", dst, p)

                            def s2a_B(ct):
                                rw = raw[ct % 2]
                                x_c = xc[ct % 2]
                                for hv in range(2):
                                    o = hv * 2048
                                    a_ = acc[hv]
                                    mk.ts("dve", a_, rw[:, 3 + o:3 + o + 2048], cw[:, ct * 4 + 3:ct * 4 + 4],
                                          cbv[:, ct:ct + 1], ALU.mult, ALU.add)
                                    for kk in (2, 1, 0):
                                        mk.stt("dve", a_, rw[:, kk + o:kk + o + 2048], cw[:, ct * 4 + kk:ct * 4 + kk + 1],
                                               a_, ALU.mult, ALU.add)
                                    mk.act(x_c[:, o:o + 2048], a_, AF.Silu)

                            def s2a_C(ct):
                                x_c = xc[ct % 2]
                                if ct < 20:
                                    sg = stg[ct % 2]
                                    for tb in range(4):
                                        pt = ptb[tb % 2]
                                        for j in range(8):
                                            mk.transpose(pt[:, j * 128:(j + 1) * 128],
                                                         x_c[:, (tb * 8 + j) * 128:(tb * 8 + j + 1) * 128], identb)
                                        mk.copy("act", sg[:, tb * 8:(tb + 1) * 8, :], pt.re("p (j c) -> p j c", j=8))
                                    for q4 in range(4):
                                        mk.dma(XSBv[:, q4 * 8:(q4 + 1) * 8, ct * 128:(ct + 1) * 128], sg[:, q4 * 8:(q4 + 1) * 8, :])
                                if ct >= 16:
                                    mk.dma(U(BCF)[(ct - 16) * 128:(ct - 15) * 128, :], x_c)

                            s2a_A(0)
                            for ct in range(24):
                                if ct + 1 < 24:
                                    s2a_A(ct + 1)
                                s2a_B(ct)
                                s2a_C(ct)
                            mk.barrier()
                        if upto >= 2.6:
                            with ExitStack() as es:
                                pf = PRot(es, 6, "pf2d")
                                Wq = [mk.sb(es, "Wq%d" % i, [128, 8, 512], BF16) for i in range(2)]
                                qst = [mk.sb(es, "qst%d" % i, [128, TO], BF16) for i in range(2)]
                                kst = [[mk.sb(es, "kst%d_%d" % (i, g), [128, TE], BF16) for g in range(3)] for i in range(2)]
                                pieces = [(0, 128)] + [(128 + i * 512, 512) for i in range(4)]
                                wi = 0
                                import os
                                SK = os.environ.get('K_SKIP', '')
                                for g, d in enumerate(DILS if 'q' not in SK else ()):
                                    for fb in range(2):
                                        W = Wq[wi % 2]
                                        wi += 1
                                        wload(W, w_in, 5152 + g * 1024 + fb * 512, 512)
                                        for fi in range(4):
                                            ft = fb * 4 + fi
                                            qs = qst[ft % 2]
                                            for (u0, n) in pieces:
                                                p = pf()
                                                for k in range(8):
                                                    mk.mm(p[:, 0:n], W[:, k, fi * 128:(fi + 1) * 128], hT[:, k, U0 + u0:U0 + u0 + n],
                                                          start=(k == 0), stop=(k == 7))
                                                mk.copy(evac_eng(), qs[:, u0:u0 + n], p[:, 0:n])
                                            mk.dma(U(Qd)[g, ft * 128:(ft + 1) * 128, 0:1024], qs[:, 0:1024])
                                            mk.dma(U(Qd)[g, ft * 128:(ft + 1) * 128, 1024:TO], qs[:, 1024:TO])
                                for fb in range(2 if 'k' not in SK else 0):
                                    W = Wq[wi % 2]
                                    wi += 1
                                    wload(W, w_in, 8224 + fb * 512, 512)
                                    for fi in range(4):
                                        ft = fb * 4 + fi
                                        for tt in range(8):
                                            p = pf()
                                            for k in range(8):
                                                mk.mm(p, W[:, k, fi * 128:(fi + 1) * 128], hT[:, k, tt * 512:(tt + 1) * 512],
                                                      start=(k == 0), stop=(k == 7))
                                            mk.copy(evac_eng(), kst[ft % 2][0][:, tt * 512:(tt + 1) * 512], p)
                                        mk.dma(U(Kd)[0, ft * 128:(ft + 1) * 128, :], kst[ft % 2][0])
                                Wv = mk.sb(es, "Wv", [128, 8, 1024], BF16)
                                wload(Wv[:, :, 0:512], w_in, 9248, 512)
                                wload(Wv[:, :, 512:1024], w_in, 9248 + 512, 512)
                                vst = [mk.sb(es, "vst%d" % i, [128, 16, 66], BF16) for i in range(2)]
                                for v_ in vst:
                                    mk.memset("pool", v_, 0.0)
                                for t in range(32 if 'v' not in SK else 0):
                                    vs = vst[t % 2]
                                    for h2 in range(2):
                                        p = pf()
                                        for k in range(8):
                                            mk.mm(p, hT[:, k, t * 128:(t + 1) * 128], Wv[:, k, h2 * 512:(h2 + 1) * 512],
                                                  start=(k == 0), stop=(k == 7))
                                        dst = vs[:, h2 * 8:(h2 + 1) * 8, 0:64]
                                        src = p.re("p (h d) -> p h d", h=8)
                                        if t < 16:
                                            mk.act(dst, src, AF.Identity, scale=flag)
                                        else:
                                            mk.copy(evac_eng(), dst, src)
                                    if t < 16:
                                        mk.copy("dve", vs[:, :, 64:65], BL(flag, 1, [128, 16, 1]))
                                    else:
                                        mk.memset("dve", vs[:, :, 64:65], 1.0)
                                    mk.dma(U(VA)[t * 128:(t + 1) * 128, :], vs.re("p h d -> p (h d)"))
                                mk.barrier()
            if upto >= 3:
                with ExitStack() as es:
                    p_acs = mk.ps(es, "p_acs", [128, 512], F32)
                    p_cb = mk.ps(es, "p_cb", [128, 512], F32)
                    p_seg = [mk.ps(es, "p_seg%d" % i, [128, 512], F32) for i in range(2)]
                    p_yd = mk.ps(es, "p_yd", [128, 512], F32)
                    p_yo = mk.ps(es, "p_yo", [128, 512], F32)
                    p_st = mk.ps(es, "p_st", [128, 512], F32)
                    p_t = mk.ps(es, "p_t", [128, 1024], BF16)
                    H = mk.sb(es, "H", [128, 2048], F32)
                    Hb = mk.sb(es, "Hb", [128, 2048], BF16)
                    mk.memset("dve", H, 0.0)
                    mk.memset("pool", Hb, 0.0)
                    dsk_b = mk.sb(es, "dsk_b", [128, 32], F32)
                    ssmg_b = mk.sb(es, "ssmg_b", [128, 2048], F32)
                    mk.dma(dsk_b, ROW(dskip, 32))
                    mk.dma(ssmg_b, ROW(ssmg, 2048))
                    xsb = [mk.sb(es, "xsb%d" % i, [128, 2560], BF16) for i in range(3)]
                    bfm = [mk.sb(es, "bfm%d" % i, [128, 4, 128], BF16) for i in range(2)]
                    cfm = [mk.sb(es, "cfm%d" % i, [128, 4, 128], BF16) for i in range(2)]
                    ztl = [mk.sb(es, "ztl%d" % i, [128, 2048], F32) for i in range(3)]
                    acs_l = [mk.sb(es, "acs_sb%d" % i, [128, 64], F32) for i in range(2)]
                    dend_l = [mk.sb(es, "dend%d" % i, [128, 32], F32) for i in range(2)]
                    wgt_l = [mk.sb(es, "wgt%d" % i, [128, 32], F32) for i in range(2)]
                    eacs_l = [mk.sb(es, "eacs%d" % i, [128, 32], F32) for i in range(2)]
                    cdec_l = [mk.sb(es, "cdec%d" % i, [128, 32], F32) for i in range(2)]
                    xw_l = [mk.sb(es, "xw%d" % i, [128, 2048], BF16) for i in range(2)]
                    xdt_l = [mk.sb(es, "xdt%d" % i, [128, 2048], BF16) for i in range(2)]
                    CBm_l = [mk.sb(es, "CBm%d" % i, [128, 512], F32) for i in range(2)]
                    Dm_l = [mk.sb(es, "Dm%d" % i, [128, 4, 128], F32) for i in range(2)]
                    Lt_l = [mk.sb(es, "Lt%d" % i, [128, 512], F32) for i in range(2)]
                    Mt_l = [mk.sb(es, "Mt%d" % i, [128, 4, 128], BF16) for i in range(3)]
                    yacc_l = [mk.sb(es, "yacc%d" % i, [128, 2048], F32) for i in range(2)]
                    tmpg_l = [mk.sb(es, "tmpg%d" % i, [128, 512], F32) for i in range(2)]
                    tmp2 = mk.sb(es, "tmp2", [128, 2048], F32)
                    junk2 = mk.sb(es, "junk2", [128, 2048], F32)
                    ss3 = mk.sb(es, "ss3", [128, 4], F32)
                    ynb = mk.sb(es, "ynb", [128, 2048], BF16)
                    ynT = [mk.sb(es, "ynT%d" % i, [128, 16, 128], BF16) for i in range(2)]
                    BCFv = U(BCF)
                    YNTv = U(YNT).re("(k p) t -> p k t", p=128)

                    def xs3_of(c):
                        return xsb[c % 3][:, 0:2048].re("p (h d) -> p h d", h=32)

                    def P1(c):
                        own = c >= OT0
                        xs_ = xsb[c % 3]
                        mk.dma(xs_, U(XSB)[c * 128:(c + 1) * 128, :])
                        bf_ = bfm[c % 2]
                        mk.dma(bf_, BCFv[0:512, c * 128:(c + 1) * 128].re("(g n) s -> n g s", g=4))
                        if own:
                            cf_ = cfm[c % 2]
                            mk.dma(cf_, BCFv[512:1024, c * 128:(c + 1) * 128].re("(g n) s -> n g s", g=4))
                            mk.dma(ztl[c % 3], U(ZS)[(c - OT0) * 128:(c - OT0 + 1) * 128, :])
                        dtA_c = dtAall[:, c, :]
                        dt_c = dtall[:, c, :]
                        acs_sb, dend, wgt, cdec = acs_l[c % 2], dend_l[c % 2], wgt_l[c % 2], cdec_l[c % 2]
                        mk.mm(p_acs[:, 0:32], tri, dtA_c)
                        mk.mm(p_acs[:, 32:64], ones, dtA_c)
                        mk.copy("act", acs_sb, p_acs[:, 0:64])
                        mk.tt("dve", dend, acs_sb[:, 32:64], acs_sb[:, 0:32], ALU.subtract)
                        mk.act(dend, dend, AF.Exp)
                        mk.act(cdec, acs_sb[:, 32:64], AF.Exp)
                        mk.tt("dve", wgt, dt_c, dend, ALU.mult)
                        mk.tt("dve", xw_l[c % 2].re("p (h d) -> p h d", h=32), xs3_of(c), BL(wgt, 2, [128, 32, 64]), ALU.mult)
                        if own:
                            mk.act(eacs_l[c % 2], acs_sb[:, 0:32], AF.Exp)
                            mk.tt("dve", xdt_l[c % 2].re("p (h d) -> p h d", h=32), xs3_of(c), BL(dt_c, 2, [128, 32, 64]), ALU.mult)
                            for g in range(4):
                                mk.mm(p_cb[:, g * 128:(g + 1) * 128], bf_[:, g, :], cf_[:, g, :])
                            mk.tt("dve", CBm_l[c % 2].re("p (g l) -> p g l", g=4), p_cb.re("p (g l) -> p g l", g=4),
                                  BL(tri, 1, [128, 4, 128]), ALU.mult)

                    def P2(c, u):
                        g = u // 2
                        dtA_c = dtAall[:, c, :]
                        Dm, Lt, Mt, ps_ = Dm_l[u % 2], Lt_l[u % 2], Mt_l[u % 3], p_seg[u % 2]
                        mk.tt("pool", Dm, BL(tri, 1, [128, 4, 128]), BL(dtA_c[:, u * 4:(u + 1) * 4], 2, [128, 4, 128]), ALU.mult)
                        mk.mm(ps_, ustr, Dm.re("p h l -> p (h l)"))
                        mk.act(Lt, ps_, AF.Exp)
                        mk.tt("dve", Mt, Lt.re("p (h l) -> p h l", h=4),
                              BL(CBm_l[c % 2][:, g * 128:(g + 1) * 128], 1, [128, 4, 128]), ALU.mult)

                    def P3(c, u):
                        g, hf = divmod(u, 2)
                        Mt = Mt_l[u % 3]
                        xdt = xdt_l[c % 2]
                        for hh in range(4):
                            h = u * 4 + hh
                            mk.mm(p_yd[:, (hf * 4 + hh) * 64:(hf * 4 + hh + 1) * 64], Mt[:, hh, :], xdt[:, h * 64:(h + 1) * 64])
                        if hf == 1:
                            tmpg = tmpg_l[g % 2]
                            mk.mm(p_yo, cfm[c % 2][:, g, :], Hb[:, g * 512:(g + 1) * 512])
                            mk.tt("dve", tmpg.re("p (h d) -> p h d", h=8), p_yo.re("p (h d) -> p h d", h=8),
                                  BL(eacs_l[c % 2][:, g * 8:(g + 1) * 8], 2, [128, 8, 64]), ALU.mult)
                            mk.tt("dve", yacc_l[c % 2][:, g * 512:(g + 1) * 512], tmpg, p_yd, ALU.add)

                    def P4(c):
                        xs_ = xsb[c % 3]
                        xw = xw_l[c % 2]
                        cdec = cdec_l[c % 2]
                        H3 = H.re("p (h d) -> p h d", h=32)
                        mk.tt("dve", H3, H3, BL(cdec, 2, [128, 32, 64]), ALU.mult)
                        pst = [p_st, p_yo]
                        def st_mm(g):
                            mk.mm(pst[g % 2], xs_[:, 2048 + g * 128:2048 + (g + 1) * 128], xw[:, g * 512:(g + 1) * 512])

                        st_mm(0)
                        st_mm(1)
                        for g in range(4):
                            Hg = H[:, g * 512:(g + 1) * 512]
                            mk.tt("dve", Hg, Hg, pst[g % 2], ALU.add)
                            if g + 2 < 4:
                                st_mm(g + 2)
                        if c == 15:
                            mk.act(H, H, AF.Identity, scale=flag)
                        mk.copy("act", Hb, H)

                    def P5a(c):
                        yacc = yacc_l[c % 2]
                        mk.tt("pool", tmp2.re("p (h d) -> p h d", h=32), xs3_of(c), BL(dsk_b, 2, [128, 32, 64]), ALU.mult)
                        mk.tt("dve", yacc, yacc, tmp2, ALU.add)
                        mk.tt("dve", yacc, yacc, ztl[c % 3], ALU.mult)
                        r = rstd_of(yacc, ss3, junk2, 2048)
                        mk.stt("dve", ynb, yacc, r, ssmg_b, ALU.mult, ALU.mult)

                    def P5b(c):
                        j = c - OT0
                        yT = ynT[c % 2]
                        for grp in range(2):
                            for k in range(8):
                                kk = grp * 8 + k
                                mk.transpose(p_t[:, k * 128:(k + 1) * 128], ynb[:, kk * 128:(kk + 1) * 128], identb)
                            mk.copy("act", yT[:, grp * 8:(grp + 1) * 8, :], p_t.re("p (k t) -> p k t", k=8))
                        mk.dma(YNTv[:, :, j * 128:(j + 1) * 128], yT)

                    P1(0)
                    for c in range(32):
                        own = c >= OT0
                        if c + 1 < 32 and not own:
                            P1(c + 1)
                        if own:
                            P2(c, 0)
                            P2(c, 1)
                            for u in range(8):
                                if u + 2 < 8:
                                    P2(c, u + 2)
                                P3(c, u)
                                if u == 3 and c + 1 < 32:
                                    P1(c + 1)
                                if c - 1 >= OT0:
                                    if u == 1:
                                        P5a(c - 1)
                                    if u == 5:
                                        P5b(c - 1)
                        if c < 31:
                            P4(c)
                    P5a(31)
                    P5b(31)
                    mk.barrier()
            es_dt.close()
            es_wo = ExitStack()
            es_wo.__enter__()
            Wo = mk.sb(es_wo, "Wo", [128, 24, D], BF16)
            wov = w_out.re("(k p) n -> p k n", p=128)
            for i in range(6):
                mk.dma(Wo[:, i * 4:(i + 1) * 4, :], wov[:, i * 4:(i + 1) * 4, :], q="pool")
            if upto >= 3.5:
                with ExitStack() as es:
                    relt_s = mk.sb(es, "relt_s", [32, 48], F32)
                    oh_s = mk.sb(es, "oh_s", [32, 3, 384], F32)
                    pm_s = mk.sb(es, "pm_s", [16, 3, 384], F32)
                    fu = mk.sb(es, "fu", [16, 3, 384], F32)
                    pp = mk.ps(es, "pp_e", [128, 512], F32)
                    pp2 = [mk.ps(es, "pp2_%d" % i, [128, 512], F32) for i in range(4)]
                    hk = [mk.sb(es, "hk%d" % i, [128, 256], F32) for i in range(8)]
                    est = mk.sb(es, "est", [128, 16, 256], F32)
                    mk.dma(relt_s, relt)
                    mk.dma(oh_s, onehot_d.re("g b x -> b g x"))
                    mk.dma(pm_s, pmask_d.re("g h x -> h g x"))
                    PMt = T(PM.ap, Buf())
                    for g in range(3):
                        mk.mm(pp[0:16, 0:384], relt_s[:, g * 16:(g + 1) * 16], oh_s[:, g, :])
                        mk.act(fu[:, g, :], pp[0:16, 0:384], AF.Exp)
                        mk.tt("dve", fu[:, g, :], fu[:, g, :], pm_s[:, g, :], ALU.mult)
                        mk.dma(PMt[g * 16:(g + 1) * 16, :], fu[:, g, :])
                    for g in range(3):
                        for h in range(16):
                            col = g * 16 + h
                            hk_ = hk[col % 8]
                            src = T(bass.AP(PM.ap.tensor, col * 384 + 1, [[1, 128], [128, 2], [1, 128]]), PMt.buf)
                            mk.dma(hk_.re("p (b i) -> p b i", b=2), src)
                            p2 = pp2[col % 4]
                            mk.mm(p2[:, 0:256], jrev, hk_)
                            mk.copy(evac_eng(), est[:, h, :], p2[:, 0:256])
                        mk.dma(U(ED)[g], est.re("p h x -> p (h x)"))
                    mk.barrier()
                if upto >= 4:
                    with ExitStack() as es:
                        VAg = mk.sb(es, "VAg", [128, 32, 1056], BF16)
                        VAb = [T(VAg.ap[:, kb, :], Buf("VAb%d" % kb)) for kb in range(32)]
                        Ep = [mk.sb(es, "Ep%d" % i, [128, 2, 256], F32) for i in range(3)]
                        Kp = [mk.sb(es, "Kp%d" % i, [128, TE], BF16) for i in range(2)]
                        Qp = [mk.sb(es, "Qp%d" % i, [128, TO], BF16) for i in range(2)]
                        p_s = [mk.ps(es, "p_s%d" % i, [128, 1024], F32) for i in range(3)]
                        p_o = [mk.ps(es, "p_o%d" % i, [128, 512], F32) for i in range(2)]
                        Ptl = [mk.sb(es, "Ptl%d" % i, [128, 512], F32) for i in range(3)]
                        Pbl = [mk.sb(es, "Pbl%d" % i, [128, 512], BF16) for i in range(3)]
                        ost = [mk.sb(es, "ost%d" % i, [128, 130], F32) for i in range(4)]
                        gcnt = [0]
                        for g, d in enumerate(DILS):
                            nb = 32 // d
                            mq0 = U0 // d
                            qblocks = []
                            for r in range(d):
                                m = mq0
                                while m < TE // d:
                                    mb = m // 128
                                    i0 = m % 128
                                    n = min(128 - i0, TE // d - m)
                                    qblocks.append((r, mb, i0, n))
                                    m += n
                            need = []
                            for (r, mb, i0, n) in qblocks:
                                kbc = r * nb + mb
                                for kb in ([kbc - 1] if mb >= 1 else []) + [kbc]:
                                    if kb not in need:
                                        need.append(kb)
                            for kb in need:
                                r, mb = divmod(kb, nb)
                                mk.dma(VAb[kb], UA(VA, (128 * mb * d + r) * 1056, [[d * 1056, 128], [1, 1056]]))
                            items = [(hp, qb) for hp in range(8) for qb in qblocks]
                            kq = {}

                            def load_pair(hp, g=g, kq=kq):
                                if hp in kq or hp >= 8:
                                    return
                                K_, Q_, E_ = Kp[hp % 2], Qp[hp % 2], Ep[hp % 3]
                                mk.dma(K_, U(Kd)[0, hp * 128:(hp + 1) * 128, :])
                                mk.dma(Q_[:, 0:1024], U(Qd)[g, hp * 128:(hp + 1) * 128, 0:1024])
                                mk.dma(Q_[:, 1024:TO], U(Qd)[g, hp * 128:(hp + 1) * 128, 1024:TO], par=True)
                                mk.dma(E_.re("p a x -> p (a x)"), U(ED)[g, :, hp * 512:(hp + 1) * 512])
                                kq[hp] = (K_, Q_, E_)

                            def a_qk(idx, g=g, d=d, nb=nb, mq0=mq0, items=items, kq=kq, load_pair=load_pair):
                                hp, (r, mb, i0, n) = items[idx]
                                load_pair(hp)
                                load_pair(hp + 1)
                                K_, Q_, E_ = kq[hp]
                                ps_ = p_s[idx % 3]
                                blks = [0, 1] if mb >= 1 else [0]
                                for hh in range(2):
                                    for blk in blks:
                                        c0 = hh * 512 + blk * 128
                                        ks = 128 * (mb - blk) * d + r
                                        qs0 = (128 * mb + i0 - mq0) * d + r
                                        mk.mm(ps_[:, c0:c0 + n], K_[hh * 64:(hh + 1) * 64, ks:ks + 127 * d + 1:d],
                                              Q_[hh * 64:(hh + 1) * 64, qs0:qs0 + (n - 1) * d + 1:d])

                            def a_sm(idx, items=items, kq=kq):
                                hp, (r, mb, i0, n) = items[idx]
                                E_ = kq[hp][2]
                                nbk = 2 if mb >= 1 else 1
                                ps_ = p_s[idx % 3]
                                Pt_ = Ptl[idx % 3]
                                Pb_ = Pbl[idx % 3]
                                v4 = lambda t_: t_.re("p (a b i) -> p a b i", a=2, b=2)[:, :, 0:nbk, 0:n]
                                mk.act(v4(Pt_), ps_.re("p (a b i) -> p a b i", a=2, b=4)[:, :, 0:nbk, 0:n], AF.Exp, scale=0.125)
                                Ev = E_.re("p a (b i) -> p a b i", b=2)[:, :, 0:nbk, i0:i0 + n]
                                mk.tt("dve", v4(Pb_), v4(Pt_), Ev, ALU.mult)

                            def a_pv(idx, g=g, d=d, nb=nb, mq0=mq0, items=items):
                                hp, (r, mb, i0, n) = items[idx]
                                kbc = r * nb + mb
                                blks = [0, 1] if mb >= 1 else [0]
                                nbk = len(blks)
                                Pb_ = Pbl[idx % 3]
                                po_ = p_o[idx % 2]
                                for hh in range(2):
                                    head = hp * 2 + hh
                                    for bi, blk in enumerate(blks):
                                        kb = kbc - blk
                                        c0 = (hh * 2 + blk) * 128
                                        mk.mm(po_[0:n, hh * 65:(hh + 1) * 65], Pb_[:, c0:c0 + n],
                                              VAb[kb][:, head * 66:head * 66 + 65], start=(bi == 0), stop=(bi == nbk - 1))

                            def a_out(idx, g=g, d=d, mq0=mq0, items=items):
                                hp, (r, mb, i0, n) = items[idx]
                                po_ = p_o[idx % 2]
                                os_ = ost[idx % 4]
                                mk.copy("act", os_[0:n, :], po_[0:n, 0:130])
                                u0 = (128 * mb + i0 - mq0) * d + r
                                mk.dma(UA(OA, g * TO * 1040 + u0 * 1040 + hp * 130, [[d * 1040, n], [1, 130]]), os_[0:n, :])

                            NI = len(items)
                            a_qk(0)
                            a_qk(1)
                            a_sm(0)
                            for idx in range(NI):
                                if idx + 2 < NI:
                                    a_qk(idx + 2)
                                if idx + 1 < NI:
                                    a_sm(idx + 1)
                                if idx >= 1:
                                    a_out(idx - 1)
                                a_pv(idx)
                            a_out(NI - 1)
                        mk.barrier()
            if upto >= 5:
                with ExitStack() as es:
                    oa = [[mk.sb(es, "oa%d_%d" % (i, g), [128, 1040], F32) for g in range(3)] for i in range(2)]
                    xt5 = [mk.sb(es, "xt5_%d" % i, [128, D], F32) for i in range(3)]
                    yT5 = [mk.sb(es, "yT5_%d" % i, [128, 16, 128], BF16) for i in range(2)]
                    attb = [mk.sb(es, "attb%d" % i, [128, D], BF16) for i in range(2)]
                    attT = [mk.sb(es, "attT%d" % i, [128, 8, 128], BF16) for i in range(2)]
                    rl = [mk.sb(es, "rl%d" % i, [128, 16], F32) for i in range(2)]
                    tmp5 = [mk.sb(es, "tmp5_%d" % i, [128, 512], F32) for i in range(2)]
                    xn5 = [mk.sb(es, "xn5_%d" % i, [128, D], F32) for i in range(2)]
                    p_t5 = [mk.ps(es, "p_t5_%d" % i, [128, 1024], BF16) for i in range(2)]
                    pf = PRot(es, 4, "pf5")
                    YNTv = U(YNT).re("(k p) t -> p k t", p=128)

                    def s5_prep(j):
                        o_ = oa[j % 2]
                        for g in range(3):
                            mk.dma(o_[g][:, 0:520], U(OA)[g, j * 128:(j + 1) * 128, 0:520])
                            mk.dma(o_[g][:, 520:1040], U(OA)[g, j * 128:(j + 1) * 128, 520:1040], par=True)
                        mk.dma(yT5[j % 2], YNTv[:, :, j * 128:(j + 1) * 128])
                        mk.dma(xt5[j % 3], xe[(OT0 + j) * 128:(OT0 + j + 1) * 128, :])
                        mk.tt("dve", o_[0], o_[0], o_[1], ALU.add)
                        mk.tt("dve", o_[0], o_[0], o_[2], ALU.add)
                        o3 = o_[0].re("p (h d) -> p h d", h=16)
                        rl_ = rl[j % 2]
                        ab_ = attb[j % 2]
                        mk.ts("dve", rl_, o3[:, :, 64], 1e-30, 1.0, ALU.max, ALU.mult)
                        mk.recip(rl_, rl_)
                        mk.tt("dve", ab_.re("p (h d) -> p h d", h=16), o3[:, :, 0:64], BL(rl_, 2, [128, 16, 64]), ALU.mult)
                        pt_ = p_t5[j % 2]
                        for k in range(8):
                            mk.transpose(pt_[:, k * 128:(k + 1) * 128], ab_[:, k * 128:(k + 1) * 128], identb)
                        mk.copy("act", attT[j % 2], pt_.re("p (k t) -> p k t", k=8))

                    def s5_main(j):
                        x_ = xt5[j % 3]
                        y_ = yT5[j % 2]
                        aT = attT[j % 2]
                        xn = xn5[j % 2]
                        for c2 in range(2):
                            p = pf()
                            for kc in range(24):
                                lhs = y_[:, kc, :] if kc < 16 else aT[:, kc - 16, :]
                                mk.mm(p, lhs, Wo[:, kc, c2 * 512:(c2 + 1) * 512], start=(kc == 0), stop=(kc == 23))
                            t5 = tmp5[c2]
                            mk.tt("dve", t5, p, mod0["GA1"][:, c2 * 512:(c2 + 1) * 512], ALU.mult)
                            mk.tt("pool", xn[:, c2 * 512:(c2 + 1) * 512], t5, x_[:, c2 * 512:(c2 + 1) * 512], ALU.add)
                        mk.dma(U(X1)[j * 128:(j + 1) * 128, :], xn, q="pool")

                    s5_prep(0)
                    for j in range(NOT):
                        if j + 1 < NOT:
                            s5_prep(j + 1)
                        s5_main(j)
                    mk.barrier()
            es_wo.close()
            if upto >= 6:
                ffn(0, X1, X2, NOT, mod0, False)
        if upto >= 7:
            es_l1 = ExitStack()
            with es_l1:
                mod1 = compute_mod(es_l1, 1)
                conformer(mod1)
                if upto >= 8:
                    ffn(1, X3, None, 16, mod1, True)
        mk.finish()
    return nc


_NC_CACHE = {}


def _prep_inputs(inputs):
    f = lambda a: np.ascontiguousarray(np.asarray(a, dtype=np.float32))
    x = f(inputs["x"])
    c = f(inputs["c"])
    shared = {
        "ada_w": f(inputs["ada_w"]),
        "ada_b": f(inputs["ada_b"]),
        "norm_mix_g": f(inputs["norm_mix_g"]),
        "norm_ffn_g": f(inputs["norm_ffn_g"]),
        "hy_w_in": f(inputs["hy_w_in"][0]),
        "convw": f(np.asarray(inputs["hy_conv_w"][0]).reshape(4, 24, 128).transpose(2, 1, 0).reshape(128, 96)),
        "convb": f(np.asarray(inputs["hy_conv_b"][0]).reshape(24, 128).T),
        "hy_dt_bias": f(inputs["hy_dt_bias"]),
        "hy_a_log": f(inputs["hy_a_log"]),
        "hy_d_skip": f(inputs["hy_d_skip"]),
        "hy_ssm_norm_g": f(inputs["hy_ssm_norm_g"]),
        "hy_w_out": f(inputs["hy_w_out"][0]),
        "rel_table": f(inputs["rel_table"]),
        "cv_w_pw1": f(inputs["cv_w_pw1"][0]),
        "b_pw1": f(np.asarray(inputs["cv_b_pw1"][0]).reshape(16, 128).T),
        "w_dw": f(np.asarray(inputs["cv_w_dw"][0]).reshape(31, 8, 128).transpose(2, 1, 0).reshape(128, 248)),
        "b_dw": f(np.asarray(inputs["cv_b_dw"][0]).reshape(8, 128).T),
        "ln_g": f(np.asarray(inputs["cv_ln_g"][0]).reshape(8, 128).T),
        "ln_b": f(np.asarray(inputs["cv_ln_b"][0]).reshape(8, 128).T),
        "cv_w_pw2": f(inputs["cv_w_pw2"][0]),
        "cv_b_pw2": f(inputs["cv_b_pw2"]),
        "ffn_w_gate": f(inputs["ffn_w_gate"]),
        "ffn_w_up": f(inputs["ffn_w_up"]),
        "ffn_w_down": f(inputs["ffn_w_down"]),
        "final_norm_g": f(np.asarray(inputs["final_norm_g"]).reshape(1, D)),
    }
    shared.update(host_consts())
    in_maps = []
    for core in range(8):
        b, hf = core // 2, core % 2
        xe = np.zeros((TE, D), np.float32)
        if hf == 1:
            xe[:] = x[b]
        else:
            xe[2048:] = x[b, :2048]
        m = dict(shared)
        m["xe"] = xe
        m["cb"] = f(c[b].reshape(8, 128).T)
        m["flag"] = np.full((128, 1), float(hf), np.float32)
        in_maps.append(m)
    return in_maps


def kernel(**inputs):
    if "nc" not in _NC_CACHE:
        _NC_CACHE["nc"] = build()
    nc = _NC_CACHE["nc"]
    in_maps = _prep_inputs(inputs)
    res = run_bass_kernel_spmd(nc, in_maps, core_ids=list(range(8)))
    out = np.zeros((4, 4096, D), np.float32)
    for core in range(8):
        b, hf = core // 2, core % 2
        out[b, hf * 2048:(hf + 1) * 2048] = res.results[core]["yout"]
    return out
```
